# Optimizing a Trainium2 kernel written in Bass

```python
import math
import jax, jax.numpy as jnp
from jax import lax
import numpy as np

D_MODEL = 2048
BATCH = 2
SEQ = 4096
DEPTH = 2
DEC_BATCH = 32
DEC_SEQ = 4
PAST_LEN = 8192
PAGE_SIZE = 128

N_MIXERS = 4
GROUP_W = D_MODEL // N_MIXERS
SSM_CH_PER_GROUP = 16
SSM_GROUPS = GROUP_W // SSM_CH_PER_GROUP
SSM_STATE = 64
CONV_K = 3
POOL_WINDOWS = (2, 4, 8, 16)
POOL_CH = GROUP_W // len(POOL_WINDOWS)
POOL_BUF = max(POOL_WINDOWS) - 1
HEAD_DIM = 64
NSA_HEADS = GROUP_W // HEAD_DIM
KV_HEADS = 2
Q_PER_KV = NSA_HEADS // KV_HEADS
BLOCK = 64
TOP_N = 16
WINDOW = 512
Q_BLOCK = 128
N_BRANCH_KV = 6
KV_COLS = N_BRANCH_KV * KV_HEADS * HEAD_DIM
IN_COLS = 6 * GROUP_W + KV_COLS + 3 * NSA_HEADS
D_FF = 4 * D_MODEL
ALPHA = (2.0 * DEPTH) ** 0.25
BETA = (8.0 * DEPTH) ** -0.25
NEG_INF = -1e30
FORCED_SCORE = Q_PER_KV + 1.0
LN_EPS = 1e-5

kernel_name = 'hybrid_s5_conv_pool_nsa_step'


def layer_norm(x, g, b):
    xf = x.astype(jnp.float32)
    mu = jnp.mean(xf, axis=-1, keepdims=True)
    var = jnp.mean(jnp.square(xf - mu), axis=-1, keepdims=True)
    y = (xf - mu) * lax.rsqrt(var + LN_EPS) * g.astype(jnp.float32) + b.astype(jnp.float32)
    return y.astype(x.dtype)


def _ssm_combine(e1, e2):
    a1r, a1i, b1r, b1i = e1
    a2r, a2i, b2r, b2i = e2
    ar = a2r * a1r - a2i * a1i
    ai = a2r * a1i + a2i * a1r
    br = a2r * b1r - a2i * b1i + b2r
    bi = a2r * b1i + a2i * b1r + b2i
    return (ar, ai, br, bi)


def s5_mixer(u, h0_re, h0_im, a_re, a_im, b_re, b_im, c_re, c_im, d, log_dt, w_glu, b_glu):
    f32 = jnp.float32
    B, T, _ = u.shape
    a_re = a_re.astype(f32)
    a_im = a_im.astype(f32)
    dt = jnp.exp(log_dt.astype(f32))[:, None]
    mag = jnp.exp(dt * a_re)
    ab_re = mag * jnp.cos(dt * a_im)
    ab_im = mag * jnp.sin(dt * a_im)
    den = a_re * a_re + a_im * a_im
    nr = ab_re - 1.0
    f_re = (nr * a_re + ab_im * a_im) / den
    f_im = (ab_im * a_re - nr * a_im) / den
    b_re = b_re.astype(f32)
    b_im = b_im.astype(f32)
    bb_re = f_re[..., None] * b_re - f_im[..., None] * b_im
    bb_im = f_re[..., None] * b_im + f_im[..., None] * b_re
    uf = u.astype(f32)
    ug = uf.reshape(B, T, SSM_GROUPS, SSM_CH_PER_GROUP)
    bu_re = jnp.einsum('gnc,btgc->btgn', bb_re, ug)
    bu_im = jnp.einsum('gnc,btgc->btgn', bb_im, ug)
    shp = bu_re.shape
    elems = (jnp.broadcast_to(ab_re, shp), jnp.broadcast_to(ab_im, shp), bu_re, bu_im)
    acc_re, acc_im, s_re, s_im = lax.associative_scan(_ssm_combine, elems, axis=1)
    h0r = h0_re.astype(f32)[:, None]
    h0i = h0_im.astype(f32)[:, None]
    h_re = s_re + acc_re * h0r - acc_im * h0i
    h_im = s_im + acc_re * h0i + acc_im * h0r
    y = (jnp.einsum('gcn,btgn->btgc', c_re.astype(f32), h_re)
         - jnp.einsum('gcn,btgn->btgc', c_im.astype(f32), h_im))
    y = y.reshape(B, T, GROUP_W) + d.astype(f32) * uf
    y = jax.nn.gelu(y)
    y = y * jax.nn.sigmoid(y @ w_glu.astype(f32) + b_glu.astype(f32))
    return y.astype(u.dtype), h_re[:, -1], h_im[:, -1]


def short_conv_mixer(g_b, g_c, v, prev, conv_w, conv_b):
    T = v.shape[1]
    cv = g_c * v
    ext = jnp.concatenate([prev.astype(cv.dtype), cv], axis=1)
    conv = conv_b
    for j in range(CONV_K):
        conv = conv + conv_w[j] * ext[:, j:j + T]
    return g_b * conv, ext[:, -(CONV_K - 1):]


def pool_mixer(u, prev, n_prev_valid, pool_w, pool_scale):
    f32 = jnp.float32
    B, T, C = u.shape
    ext = jnp.concatenate([prev.astype(u.dtype), u], axis=1)
    cs = jnp.concatenate([jnp.zeros((B, 1, C), f32), jnp.cumsum(ext.astype(f32), axis=1)], axis=1)
    hi = cs[:, POOL_BUF + 1:]
    t = jnp.arange(T)
    outs = []
    for gi, w in enumerate(POOL_WINDOWS):
        sl = slice(gi * POOL_CH, (gi + 1) * POOL_CH)
        lo = cs[:, POOL_BUF + 1 - w:POOL_BUF + 1 - w + T, sl]
        cnt = jnp.minimum(t + 1 + n_prev_valid, w).astype(f32)[None, :, None]
        outs.append((hi[..., sl] - lo) / cnt)
    pooled = jnp.concatenate(outs, axis=-1) - u.astype(f32)
    y = jnp.einsum('btgc,gcd->btgd', pooled.reshape(B, T, len(POOL_WINDOWS), POOL_CH), pool_w.astype(f32))
    y = y.reshape(B, T, C) * pool_scale.astype(f32)
    return y.astype(u.dtype), ext[:, -POOL_BUF:]


def compress_blocks(k, w):
    B, L, G, D = k.shape
    kb = k.reshape(B, L // BLOCK, BLOCK, G, D)
    return jnp.einsum('bjkgd,kd->bjgd', kb, w)


def nsa_compressed_branch(q, q_pos, k_cmp, v_cmp, w_ck, w_cv):
    B, T, H, D = q.shape
    kc = compress_blocks(k_cmp, w_ck)
    vc = compress_blocks(v_cmp, w_cv)
    NB = kc.shape[1]
    qg = q.reshape(B, T, KV_HEADS, Q_PER_KV, D)
    s = jnp.einsum('btgrd,bjgd->bgrtj', qg, kc).astype(jnp.float32) * (HEAD_DIM ** -0.5)
    j = jnp.arange(NB)
    complete = ((j + 1) * BLOCK - 1)[None, :] <= q_pos[:, None]
    s = jnp.where(complete, s, NEG_INF)
    p = jnp.where(complete, jax.nn.softmax(s, axis=-1), 0.0)
    o = jnp.einsum('bgrtj,bjgd->btgrd', p.astype(vc.dtype), vc).reshape(B, T, H, D)
    imp = jnp.sum(p, axis=2)
    cur = q_pos // BLOCK
    forced = (j[None, :] == 0) | (j[None, :] == cur[:, None]) | (j[None, :] == cur[:, None] - 1)
    started = (j * BLOCK)[None, :] <= q_pos[:, None]
    score = jnp.where(forced, FORCED_SCORE, jnp.where(started, imp, -1.0))
    _, idx = lax.top_k(score, min(TOP_N, NB))
    return o, idx


def nsa_selected_branch(q, q_pos, k_sel, v_sel, idx):
    B, T, H, D = q.shape
    L = k_sel.shape[1]
    kb = k_sel.reshape(B, L // BLOCK, BLOCK, KV_HEADS, D).transpose(0, 3, 1, 2, 4)
    vb = v_sel.reshape(B, L // BLOCK, BLOCK, KV_HEADS, D).transpose(0, 3, 1, 2, 4)
    qb = min(Q_BLOCK, T)
    nq = T // qb
    n_sel = idx.shape[-1]
    gather = jax.vmap(jax.vmap(lambda tbl, ii: tbl[ii]))

    def one(args):
        qc, pc, ic = args
        kg = gather(kb, ic)
        vg = gather(vb, ic)
        kpos = ic[..., None] * BLOCK + jnp.arange(BLOCK)
        s = jnp.einsum('bqgrd,bgqnkd->bgrqnk', qc.reshape(B, qb, KV_HEADS, Q_PER_KV, D), kg)
        s = s.astype(jnp.float32) * (HEAD_DIM ** -0.5)
        mask = (kpos <= pc[None, None, :, None, None])[:, :, None]
        s = jnp.where(mask, s, NEG_INF).reshape(B, KV_HEADS, Q_PER_KV, qb, n_sel * BLOCK)
        p = jax.nn.softmax(s, axis=-1).reshape(B, KV_HEADS, Q_PER_KV, qb, n_sel, BLOCK)
        o = jnp.einsum('bgrqnk,bgqnkd->bqgrd', p.astype(vg.dtype), vg)
        return o.reshape(B, qb, H, D)

    qs = q.reshape(B, nq, qb, H, D).transpose(1, 0, 2, 3, 4)
    ps = q_pos.reshape(nq, qb)
    ids = idx.reshape(B, KV_HEADS, nq, qb, n_sel).transpose(2, 0, 1, 3, 4)
    out = lax.map(one, (qs, ps, ids))
    return out.transpose(1, 0, 2, 3, 4).reshape(B, T, H, D)


def nsa_window_branch(q, q_pos, k_ext, v_ext, kpos_ext, n_prev):
    B, T, H, D = q.shape
    qb = min(Q_BLOCK, T)
    nq = T // qb
    span = n_prev + qb

    def one(args):
        c, qc, pc = args
        start = c * qb
        kc = lax.dynamic_slice_in_dim(k_ext, start, span, axis=1)
        vc = lax.dynamic_slice_in_dim(v_ext, start, span, axis=1)
        kp = lax.dynamic_slice_in_dim(kpos_ext, start, span, axis=0)
        s = jnp.einsum('bqgrd,bkgd->bgrqk', qc.reshape(B, qb, KV_HEADS, Q_PER_KV, D), kc)
        s = s.astype(jnp.float32) * (HEAD_DIM ** -0.5)
        rel = pc[:, None] - kp[None, :]
        mask = (rel >= 0) & (rel < WINDOW) & (kp[None, :] >= 0)
        p = jax.nn.softmax(jnp.where(mask, s, NEG_INF), axis=-1)
        o = jnp.einsum('bgrqk,bkgd->bqgrd', p.astype(vc.dtype), vc)
        return o.reshape(B, qb, H, D)

    qs = q.reshape(B, nq, qb, H, D).transpose(1, 0, 2, 3, 4)
    ps = q_pos.reshape(nq, qb)
    out = lax.map(one, (jnp.arange(nq), qs, ps))
    return out.transpose(1, 0, 2, 3, 4).reshape(B, T, H, D)


def trunk_layer(x, lp, pos0, h0_re, h0_im, conv_prev, pool_prev, pool_prev_valid, past_kv, win_prev):
    B, T, _ = x.shape
    q_pos = pos0 + jnp.arange(T)
    h = x @ lp['w_in']
    sizes = (GROUP_W, GROUP_W, GROUP_W, GROUP_W, GROUP_W, GROUP_W, KV_COLS)
    cuts = [int(c) for c in np.cumsum(sizes)]
    u_ssm, g_b, g_c, v_conv, u_pool, q, kv, g_nsa = jnp.split(h, cuts, axis=-1)
    y_a, h_re, h_im = s5_mixer(u_ssm, h0_re, h0_im, lp['ssm_a_re'], lp['ssm_a_im'], lp['ssm_b_re'],
                               lp['ssm_b_im'], lp['ssm_c_re'], lp['ssm_c_im'], lp['ssm_d'],
                               lp['ssm_log_dt'], lp['ssm_w_glu'], lp['ssm_b_glu'])
    y_b, conv_new = short_conv_mixer(g_b, g_c, v_conv, conv_prev, lp['conv_w'], lp['conv_b'])
    y_c, pool_new = pool_mixer(u_pool, pool_prev, pool_prev_valid, lp['pool_w'], lp['pool_scale'])
    q = q.reshape(B, T, NSA_HEADS, HEAD_DIM)
    kv = kv.reshape(B, T, N_BRANCH_KV, KV_HEADS, HEAD_DIM)
    gates = jax.nn.sigmoid(g_nsa.astype(jnp.float32)).reshape(B, T, NSA_HEADS, 3).astype(x.dtype)
    rows = kv[:, :, :4]
    full = jnp.concatenate([past_kv.astype(rows.dtype), rows], axis=1)
    pad = (-full.shape[1]) % BLOCK
    full = jnp.pad(full, ((0, 0), (0, pad), (0, 0), (0, 0), (0, 0)))
    win_ext = jnp.concatenate([win_prev.astype(kv.dtype), kv[:, :, 4:]], axis=1)
    n_prev = win_prev.shape[1]
    kpos_ext = jnp.arange(n_prev + T) + (pos0 - n_prev)
    o_cmp, idx = nsa_compressed_branch(q, q_pos, full[:, :, 0], full[:, :, 1], lp['nsa_w_cmp_k'], lp['nsa_w_cmp_v'])
    o_sel = nsa_selected_branch(q, q_pos, full[:, :, 2], full[:, :, 3], idx)
    o_win = nsa_window_branch(q, q_pos, win_ext[:, :, 0], win_ext[:, :, 1], kpos_ext, n_prev)
    y_d = (gates[..., 0:1] * o_cmp + gates[..., 1:2] * o_sel + gates[..., 2:3] * o_win).reshape(B, T, GROUP_W)
    mix = jnp.concatenate([y_a, y_b, y_c, y_d], axis=-1) @ lp['w_out']
    x = layer_norm(ALPHA * x + mix, lp['ln1_g'], lp['ln1_b'])
    ff = jnp.square(jax.nn.relu(x @ lp['w_up'])) @ lp['w_down']
    x = layer_norm(ALPHA * x + ff, lp['ln2_g'], lp['ln2_b'])
    n_keep = min(n_prev, pos0 + T)
    return x, (rows, win_ext[:, -n_keep:], h_re, h_im, conv_new, pool_new)


def setup_inputs(seed: int = 0) -> dict:
    key = jax.random.key(seed)
    k = jax.random.split(key, 32)
    f32 = jnp.float32

    def nrm(i, shape, scale):
        return scale * jax.random.normal(k[i], shape, f32)

    n_pages = PAST_LEN // PAGE_SIZE
    n_used = DEC_BATCH * n_pages
    n_pool = n_used + max(1, n_used // 4)
    w_buf = min(WINDOW, PAST_LEN)
    page_table = jax.random.permutation(k[8], n_pool)[:n_used].reshape(DEC_BATCH, n_pages).astype(jnp.int32)
    a_im = jnp.broadcast_to(math.pi * jnp.arange(SSM_STATE, dtype=f32), (DEPTH, SSM_GROUPS, SSM_STATE))
    return {
        'x_prompt': nrm(0, (BATCH, SEQ, D_MODEL), 1.0),
        'x_sample': nrm(1, (DEC_BATCH, DEC_SEQ, D_MODEL), 1.0),
        'cache_nsa_kv': nrm(2, (DEPTH, n_pool, PAGE_SIZE, 4, KV_HEADS, HEAD_DIM), 1.0),
        'cache_win_kv': nrm(3, (DEPTH, DEC_BATCH, w_buf, 2, KV_HEADS, HEAD_DIM), 1.0),
        'state_ssm_re': nrm(4, (DEPTH, DEC_BATCH, SSM_GROUPS, SSM_STATE), 0.3),
        'state_ssm_im': nrm(5, (DEPTH, DEC_BATCH, SSM_GROUPS, SSM_STATE), 0.3),
        'state_conv': nrm(6, (DEPTH, DEC_BATCH, CONV_K - 1, GROUP_W), 1.0),
        'state_pool': nrm(7, (DEPTH, DEC_BATCH, POOL_BUF, GROUP_W), 1.0),
        'page_table': page_table,
        'w_in': nrm(9, (DEPTH, D_MODEL, IN_COLS), D_MODEL ** -0.5),
        'ssm_a_re': -0.5 * jnp.exp(nrm(10, (DEPTH, SSM_GROUPS, SSM_STATE), 0.02)),
        'ssm_a_im': a_im,
        'ssm_b_re': nrm(11, (DEPTH, SSM_GROUPS, SSM_STATE, SSM_CH_PER_GROUP), (2 * SSM_CH_PER_GROUP) ** -0.5),
        'ssm_b_im': nrm(12, (DEPTH, SSM_GROUPS, SSM_STATE, SSM_CH_PER_GROUP), (2 * SSM_CH_PER_GROUP) ** -0.5),
        'ssm_c_re': nrm(13, (DEPTH, SSM_GROUPS, SSM_CH_PER_GROUP, SSM_STATE), (2 * SSM_STATE) ** -0.5),
        'ssm_c_im': nrm(14, (DEPTH, SSM_GROUPS, SSM_CH_PER_GROUP, SSM_STATE), (2 * SSM_STATE) ** -0.5),
        'ssm_d': nrm(15, (DEPTH, GROUP_W), 1.0),
        'ssm_log_dt': jax.random.uniform(k[16], (DEPTH, SSM_GROUPS), f32, math.log(1e-3), math.log(1e-1)),
        'ssm_w_glu': nrm(17, (DEPTH, GROUP_W, GROUP_W), GROUP_W ** -0.5),
        'ssm_b_glu': nrm(18, (DEPTH, GROUP_W), 0.01),
        'conv_w': nrm(19, (DEPTH, CONV_K, GROUP_W), CONV_K ** -0.5),
        'conv_b': nrm(20, (DEPTH, GROUP_W), 0.01),
        'pool_w': nrm(21, (DEPTH, len(POOL_WINDOWS), POOL_CH, POOL_CH), POOL_CH ** -0.5),
        'pool_scale': 1.0 + nrm(22, (DEPTH, GROUP_W), 0.1),
        'nsa_w_cmp_k': (1.0 + nrm(23, (DEPTH, BLOCK, HEAD_DIM), 0.1)) / BLOCK,
        'nsa_w_cmp_v': (1.0 + nrm(24, (DEPTH, BLOCK, HEAD_DIM), 0.1)) / BLOCK,
        'w_out': nrm(25, (DEPTH, D_MODEL, D_MODEL), BETA * D_MODEL ** -0.5),
        'ln1_g': 1.0 + nrm(26, (DEPTH, D_MODEL), 0.05),
        'ln1_b': nrm(27, (DEPTH, D_MODEL), 0.01),
        'w_up': nrm(28, (DEPTH, D_MODEL, D_FF), D_MODEL ** -0.5),
        'w_down': nrm(29, (DEPTH, D_FF, D_MODEL), BETA * D_FF ** -0.5),
        'ln2_g': 1.0 + nrm(30, (DEPTH, D_MODEL), 0.05),
        'ln2_b': nrm(31, (DEPTH, D_MODEL), 0.01),
    }


def reference(x_prompt, x_sample, cache_nsa_kv, cache_win_kv, state_ssm_re, state_ssm_im, state_conv,
              state_pool, page_table, w_in, ssm_a_re, ssm_a_im, ssm_b_re, ssm_b_im, ssm_c_re, ssm_c_im,
              ssm_d, ssm_log_dt, ssm_w_glu, ssm_b_glu, conv_w, conv_b, pool_w, pool_scale, nsa_w_cmp_k,
              nsa_w_cmp_v, w_out, ln1_g, ln1_b, w_up, w_down, ln2_g, ln2_b):
    B, T, _ = x_prompt.shape
    Bd = x_sample.shape[0]
    n_pages = page_table.shape[1]
    past_len = n_pages * cache_nsa_kv.shape[2]
    xp = x_prompt
    xs = x_sample
    kv_p, kv_s, win_p, win_s = [], [], [], []
    sre_p, sim_p, sre_s, sim_s = [], [], [], []
    conv_p, conv_s, pool_p, pool_s = [], [], [], []
    for l in range(DEPTH):
        lp = {'w_in': w_in[l], 'ssm_a_re': ssm_a_re[l], 'ssm_a_im': ssm_a_im[l], 'ssm_b_re': ssm_b_re[l],
              'ssm_b_im': ssm_b_im[l], 'ssm_c_re': ssm_c_re[l], 'ssm_c_im': ssm_c_im[l], 'ssm_d': ssm_d[l],
              'ssm_log_dt': ssm_log_dt[l], 'ssm_w_glu': ssm_w_glu[l], 'ssm_b_glu': ssm_b_glu[l],
              'conv_w': conv_w[l], 'conv_b': conv_b[l], 'pool_w': pool_w[l], 'pool_scale': pool_scale[l],
              'nsa_w_cmp_k': nsa_w_cmp_k[l], 'nsa_w_cmp_v': nsa_w_cmp_v[l], 'w_out': w_out[l],
              'ln1_g': ln1_g[l], 'ln1_b': ln1_b[l], 'w_up': w_up[l], 'w_down': w_down[l],
              'ln2_g': ln2_g[l], 'ln2_b': ln2_b[l]}
        xp, st = trunk_layer(
            xp, lp, 0,
            jnp.zeros((B, SSM_GROUPS, SSM_STATE), jnp.float32),
            jnp.zeros((B, SSM_GROUPS, SSM_STATE), jnp.float32),
            jnp.zeros((B, CONV_K - 1, GROUP_W), xp.dtype),
            jnp.zeros((B, POOL_BUF, GROUP_W), xp.dtype), 0,
            jnp.zeros((B, 0, 4, KV_HEADS, HEAD_DIM), xp.dtype),
            jnp.zeros((B, WINDOW, 2, KV_HEADS, HEAD_DIM), xp.dtype))
        kv_p.append(st[0]); win_p.append(st[1]); sre_p.append(st[2]); sim_p.append(st[3])
        conv_p.append(st[4]); pool_p.append(st[5])
        past = cache_nsa_kv[l][page_table].reshape(Bd, past_len, 4, KV_HEADS, HEAD_DIM)
        xs, st = trunk_layer(
            xs, lp, past_len, state_ssm_re[l], state_ssm_im[l], state_conv[l], state_pool[l],
            min(POOL_BUF, past_len), past, cache_win_kv[l])
        kv_s.append(st[0]); win_s.append(st[1]); sre_s.append(st[2]); sim_s.append(st[3])
        conv_s.append(st[4]); pool_s.append(st[5])
    return (xp, xs, jnp.stack(kv_p), jnp.stack(kv_s), jnp.stack(win_p), jnp.stack(win_s),
            jnp.stack(sre_p), jnp.stack(sim_p), jnp.stack(sre_s), jnp.stack(sim_s),
            jnp.stack(conv_p), jnp.stack(conv_s), jnp.stack(pool_p), jnp.stack(pool_s))
```

```python
import contextlib
import numpy as np
import concourse.bass as bass
import concourse.mybir as mybir

F32 = mybir.dt.float32
BF16 = mybir.dt.bfloat16
I32 = mybir.dt.int32
U32 = mybir.dt.uint32
ALU = mybir.AluOpType
AF = mybir.ActivationFunctionType
AX = mybir.AxisListType

COMPUTE = ("tensor", "vector", "scalar", "gpsimd")
NDMA_SEM = {"sync": 12, "gpsimd": 8}


class Tile:
    def __init__(self, name, ap, nparts=1):
        self.name = name
        self.ap = ap
        self.nparts = nparts

    def __getitem__(self, idx):
        return View(self.ap[idx], [(self, p) for p in range(self.nparts)])

    def p(self, *parts):
        return _PartProxy(self, parts)

    def all(self):
        return View(self.ap, [(self, p) for p in range(self.nparts)])


class _PartProxy:
    def __init__(self, tile, parts):
        self.tile, self.parts = tile, parts

    def __getitem__(self, idx):
        return View(self.tile.ap[idx], [(self.tile, p) for p in self.parts])


class View:
    def __init__(self, ap, res):
        self.ap = ap
        self.res = res

    def __getitem__(self, idx):
        return View(self.ap[idx], self.res)

    def rearrange(self, *a, **k):
        return View(self.ap.rearrange(*a, **k), self.res)

    def bitcast(self, dt):
        return View(self.ap.bitcast(dt), self.res)

    def map(self, f):
        return View(f(self.ap), self.res)


class _Op:
    __slots__ = ("eng", "fn", "deps", "signal", "count", "dma_sem", "dma_val", "dma_prev", "name")


class Prog:
    def __init__(self, nc, same_engine_sync=True):
        self.nc = nc
        self.ops = {e: [] for e in ("tensor", "vector", "scalar", "gpsimd", "sync")}
        self.last_w = {}
        self.readers = {}
        self.same_engine_sync = same_engine_sync
        self.dma_count = {"sync": 0, "gpsimd": 0}
        self.dma_sem_total = {}
        self.stack = contextlib.ExitStack()
        self.out_dma = []

    def sbuf(self, name, shape, dtype, nparts=1):
        t = self.stack.enter_context(self.nc.sbuf_tensor(name, list(shape), dtype))
        return Tile(name, t[:], nparts)

    def psum(self, name, shape, dtype, nparts=1):
        t = self.stack.enter_context(self.nc.psum_tensor(name, list(shape), dtype))
        return Tile(name, t[:], nparts)

    def dram(self, name, shape, dtype, kind, nparts=1, **kw):
        t = self.nc.dram_tensor(name, list(shape), dtype, kind=kind, **kw)
        return Tile(name, t.ap(), nparts)

    def _record(self, eng, fn, reads, writes, is_dma=False, name=None):
        op = _Op()
        op.eng, op.fn, op.signal, op.count, op.name = eng, fn, False, None, name
        op.dma_sem = op.dma_val = op.dma_prev = None
        idx = len(self.ops[eng])
        me = (eng, idx)
        deps = set()
        for r in reads:
            w = self.last_w.get(r)
            if w is not None:
                deps.add(w)
        for w_ in writes:
            w = self.last_w.get(w_)
            if w is not None:
                deps.add(w)
            for rd in self.readers.get(w_, ()):
                deps.add(rd)
        deps.discard(me)
        for r in reads:
            self.readers.setdefault(r, []).append(me)
        for w_ in writes:
            self.last_w[w_] = me
            self.readers[w_] = []
        if is_dma:
            k = self.dma_count[eng]
            self.dma_count[eng] += 1
            slot = (eng, k % NDMA_SEM[eng])
            prev = self.dma_sem_total.get(slot, 0)
            op.dma_sem = slot
            op.dma_prev = prev
            op.dma_val = prev + 16
            self.dma_sem_total[slot] = prev + 16
        op.deps = deps
        self.ops[eng].append(op)
        return op

    @staticmethod
    def _split(kwargs, out_names):
        reads, writes, call = [], [], {}
        for k, v in kwargs.items():
            if isinstance(v, View):
                (writes if k in out_names else reads).extend(v.res)
                call[k] = v.ap
            else:
                call[k] = v
        return reads, writes, call

    def op(self, eng, method, _extra_reads=(), _extra_writes=(), **kwargs):
        reads, writes, call = self._split(kwargs, ("out", "accum_out"))
        for v in _extra_reads:
            reads.extend(v.res)
        for v in _extra_writes:
            writes.extend(v.res)

        def fn(e, method=method, call=call):
            return getattr(e, method)(**call)

        return self._record(eng, fn, reads, writes, name=method)

    def custom(self, eng, fn, reads=(), writes=(), name=None):
        r, w = [], []
        for v in reads:
            r.extend(v.res)
        for v in writes:
            w.extend(v.res)
        return self._record(eng, fn, r, w, name=name)

    def dma(self, out, in_, eng="sync", is_output=False, **kw):
        def fn(e, o=out.ap, i=in_.ap, kw=kw):
            return e.dma_start(out=o, in_=i, **kw)

        op = self._record(eng, fn, list(in_.res), list(out.res), is_dma=True, name="dma")
        if is_output:
            self.out_dma.append(op)
        return op

    def dma_custom(self, fn, reads, writes, eng="gpsimd"):
        r, w = [], []
        for v in reads:
            r.extend(v.res)
        for v in writes:
            w.extend(v.res)
        return self._record(eng, fn, r, w, is_dma=True, name="dmac")

    def mm(self, out, lhsT, rhs, start=True, stop=True, **kw):
        return self.op("tensor", "matmul", out=out, lhsT=lhsT, rhs=rhs, start=start, stop=stop, **kw)

    def act(self, out, in_, func, eng="scalar", **kw):
        return self.op(eng, "activation", out=out, in_=in_, func=func, **kw)

    def tt(self, out, in0, in1, op, eng="vector"):
        return self.op(eng, "tensor_tensor", out=out, in0=in0, in1=in1, op=op)

    def ts(self, out, in0, s1, op0, s2=None, op1=None, eng="vector", **kw):
        if op1 is None:
            return self.op(eng, "tensor_scalar", out=out, in0=in0, scalar1=s1, scalar2=None, op0=op0, **kw)
        return self.op(eng, "tensor_scalar", out=out, in0=in0, scalar1=s1, scalar2=s2, op0=op0, op1=op1, **kw)

    def stt(self, out, in0, scalar, in1, op0, op1, eng="vector"):
        return self.op(eng, "scalar_tensor_tensor", out=out, in0=in0, scalar=scalar, in1=in1, op0=op0, op1=op1)

    def copy(self, out, in_, eng="vector"):
        if eng == "scalar":
            return self.op("scalar", "activation", out=out, in_=in_, func=AF.Copy)
        return self.op(eng, "tensor_copy", out=out, in_=in_)

    def memset(self, out, val, eng="vector"):
        return self.op(eng, "memset", _extra_writes=[out], ap=out.ap, constant=val)

    def emit(self):
        nc = self.nc
        for eng, lst in self.ops.items():
            for op in lst:
                for (de, di) in op.deps:
                    d = self.ops[de][di]
                    if d.dma_sem is None:
                        d.signal = True
        for eng, lst in self.ops.items():
            c = 0
            for op in lst:
                if op.dma_sem is None and op.signal:
                    c += 1
                    op.count = c
        sems = {}
        st = self.stack
        for e in COMPUTE + ("sync",):
            sems[e] = st.enter_context(nc.semaphore("s_" + e))
        dsems = {}
        for q, n in NDMA_SEM.items():
            for i in range(n):
                dsems[(q, i)] = st.enter_context(nc.semaphore("d_%s%d" % (q, i)))
        ops_all = self.ops
        same = self.same_engine_sync
        final = dict(self.dma_sem_total)

        def run_engine(ename, e):
            waited = {}
            for idx, op in enumerate(ops_all[ename]):
                need = {}
                for (de, di) in op.deps:
                    d = ops_all[de][di]
                    if d.dma_sem is not None:
                        key = ("D",) + d.dma_sem
                        val = d.dma_val
                    else:
                        if de == ename:
                            if ename in ("tensor", "sync"):
                                continue
                            if not same:
                                continue
                        key = ("E", de)
                        val = d.count
                    if need.get(key, 0) < val:
                        need[key] = val
                if op.dma_sem is not None and op.dma_prev > 0:
                    key = ("D",) + op.dma_sem
                    if need.get(key, 0) < op.dma_prev:
                        need[key] = op.dma_prev
                for key, val in need.items():
                    if waited.get(key, 0) >= val:
                        continue
                    waited[key] = val
                    s = dsems[key[1:]] if key[0] == "D" else sems[key[1]]
                    e.wait_ge(s, val)
                inst = op.fn(e)
                if op.dma_sem is not None:
                    inst.then_inc(dsems[op.dma_sem], 16)
                elif op.signal:
                    inst.then_inc(sems[ename], 1)
            if ename == "sync":
                for slot, val in final.items():
                    e.wait_ge(dsems[slot], val)

        with nc.allow_low_precision(reason='bf16 matmul operands by design'), nc.Block() as block:
            @block.sync
            def _(e):
                run_engine("sync", e)

            @block.gpsimd
            def _(e):
                run_engine("gpsimd", e)

            @block.tensor
            def _(e):
                run_engine("tensor", e)

            @block.vector
            def _(e):
                run_engine("vector", e)

            @block.scalar
            def _(e):
                run_engine("scalar", e)

    def close(self):
        self.stack.close()

    def stats(self):
        return {e: len(l) for e, l in self.ops.items()}

import math
from concourse.bass_utils import run_bass_kernel_spmd

D = 2048
NCOL = 3864
DFF = 8192
NCH = 512
NT = 4
SEQ = 4096
NCHUNK = SEQ // NCH
ALPHA_ = (2.0 * 2) ** 0.25
NEG = -30000.0
LSEG = 256
GC = 2.0 * math.sqrt(2.0 / math.pi)
CACHE_ROWS = 2560 * 128
NCORES = 2
DEBUG = False
NSA_LEVEL = 9


def build_program(n_layers=2, n_chunks=NCHUNK, do_sample=True, parts=("s5", "conv", "pool", "nsa", "ffn")):
    NB = 32 // NCORES
    NS = NB * 4
    nc = bass.Bass("TRN2", target_bir_lowering=False)
    P = Prog(nc)
    EI, EO, IN = "ExternalInput", "ExternalOutput", "Internal"
    d = {}
    def din(name, shape, dt=F32):
        d[name] = P.dram(name, shape, dt, EI)
        return d[name]
    def dout(name, shape, dt=F32):
        d[name] = P.dram(name, shape, dt, EO)
        return d[name]
    x_p = din("x_p", [SEQ, D]); x_s = din("x_s", [NS, D])
    cache = din("cache", [2, CACHE_ROWS, 512]); cwin = din("cwin", [2, NB, 512, 256])
    st_re = din("st_re", [2, NB, 32, 64]); st_im = din("st_im", [2, NB, 32, 64])
    st_conv = din("st_conv", [2, NB, 2, 512]); st_pool = din("st_pool", [2, NB, 15, 512])
    ptab = din("ptab", [NB, 64], I32)
    w_in = din("w_in", [2, D, NCOL]); w_out = din("w_out", [2, D, D]); w_up = din("w_up", [2, D, DFF]); w_down = din("w_down", [2, DFF, D])
    a_re = din("a_re", [2, 32, 64]); a_im = din("a_im", [2, 32, 64]); ldt = din("ldt", [2, 32])
    b_re = din("b_re", [2, 32, 64, 16]); b_im = din("b_im", [2, 32, 64, 16])
    c_re = din("c_re", [2, 32, 16, 64]); c_im = din("c_im", [2, 32, 16, 64])
    ssm_d = din("ssm_d", [2, 512]); w_glu = din("w_glu", [2, 512, 512]); b_glu = din("b_glu", [2, 512])
    conv_w = din("conv_w", [2, 3, 512]); conv_b = din("conv_b", [2, 512])
    pool_w = din("pool_w", [2, 4, 128, 128]); pool_sc = din("pool_sc", [2, 512])
    wck = din("wck", [2, 64, 64]); wcv = din("wcv", [2, 64, 64])
    ln1g = din("ln1g", [2, D]); ln1b = din("ln1b", [2, D]); ln2g = din("ln2g", [2, D]); ln2b = din("ln2b", [2, D])
    c_ident = din("c_ident", [128, 128])
    c_caus = din("c_caus", [128, 128]); c_anti = din("c_anti", [128, 128])
    c_ebig = din("c_ebig", [128, 8192])
    c_tmask = din("c_tmask", [32, 128, 3, 64])
    c_cmpbT = din("c_cmpbT", [32, 64, 128])
    c_rcnt0 = din("c_rcnt0", [128, 4, 16])
    din("c_iota", [128, 1]); din("c_ibig", [128, 254]); din("c_nb", [NS, NB, 16]); din("c_wb0", [128, 16]); din("c_selb", [NS, NB, 16]); din("c_rm", [16, 6])
    y_p = dout("y_p", [SEQ, D]); y_s = dout("y_s", [NS, D])
    kvp = dout("kvp", [2, SEQ, 512]); kvs = dout("kvs", [2, NS, 512])
    winp = dout("winp", [2, 512, 256]); wins = dout("wins", [2, NB, 512, 256])
    sre_p = dout("sre_p", [2, 32, 64]); sim_p = dout("sim_p", [2, 32, 64])
    sre_s = dout("sre_s", [2, NB, 32, 64]); sim_s = dout("sim_s", [2, NB, 32, 64])
    conv_p = dout("conv_p", [2, 2, 512]); conv_s = dout("conv_s", [2, NB, 2, 512])
    pool_p = dout("pool_p", [2, 15, 512]); pool_s = dout("pool_s", [2, NB, 15, 512])
    if DEBUG:
        dout("dbg", [NCHUNK, 128, 28, NCH]); dout("dbgs", [128, 28, NS])
    x1d = P.dram("x1d", [SEQ, D], F32, IN)
    tabd = P.dram("tabd", [2, 128, 16, 2, LSEG], F32, IN)
    ctd = P.dram("ctd", [2, 128, 16, 2, 128], F32, IN)

    WS = [P.sbuf("ws%d" % i, [128, 4096], BF16) for i in range(4)]
    XT = P.sbuf("xT", [128, 16, NCH], BF16, nparts=16)
    HT = P.sbuf("hT", [128, 28, NCH], BF16, nparts=28)
    ACC = P.sbuf("acc", [128, NT, D], F32, nparts=NT)
    TMP = P.sbuf("tmp", [128, 6144], F32, nparts=24)
    identf = P.sbuf("identf", [128, 128], F32); identb = P.sbuf("identb", [128, 128], BF16)
    caus4 = P.sbuf("caus4", [128, 4, 128], BF16); anti4 = P.sbuf("anti4", [128, 4, 128], BF16)
    ebig = P.sbuf("ebig", [128, 4096], BF16)
    BT = P.sbuf("BT", [128, 16, 2, 128], BF16)
    wglu = P.sbuf("wglu", [128, 4, 512], BF16); poolw = P.sbuf("poolw", [128, 4, 128], BF16)
    sc = P.sbuf("sc", [128, 64], F32)
    mag = P.sbuf("mag", [128, 16], F32)
    wckT = P.sbuf("wckT", [128, 64], F32); wcvT = P.sbuf("wcvT", [128, 64], F32)
    KselT = P.sbuf("KselT", [128, SEQ], BF16, nparts=32)
    Vsel1 = P.sbuf("Vsel1", [128, 32, 2, 72], BF16, nparts=32)
    KwinT = P.sbuf("KwinT", [128, 8, 128], BF16, nparts=8)
    Vwin1 = P.sbuf("Vwin1", [128, 8, 2, 72], BF16, nparts=8)
    kcT = P.sbuf("kcT", [128, 64], BF16); vcT = P.sbuf("vcT", [128, 64], BF16)
    vc1 = P.sbuf("vc1", [128, 2, 72], BF16)
    kcTz = P.sbuf("kcTz", [128, 2, 128], BF16)
    hst = P.sbuf("hst", [128, 16, 2], F32)
    convc = P.sbuf("convc", [128, 4, 2], F32); poolc = P.sbuf("poolc", [128, 4, 15], F32)
    gate = P.sbuf("gate", [128, NT, 24], F32)
    PS = [P.psum("ps%d" % i, [128, 512], F32) for i in range(8)]

    tmp = TMP
    def tv(off, shape, dt=F32):
        n = 1
        for s in shape[1:]:
            n *= s
        nw = (n + 1) // 2 if dt == BF16 else n
        assert off + nw <= 6144, (off, shape)
        pp = tmp.p(*range(off // 256, (off + nw - 1) // 256 + 1))
        if dt == BF16:
            v = pp[:, off:off + nw].bitcast(BF16)
            v = v[:, 0:n]
        else:
            v = pp[:, off:off + n]
        if shape[0] != 128:
            v = v[0:shape[0]]
        if len(shape) == 2:
            return v
        names = " ".join("a%d" % i for i in range(len(shape) - 1))
        kw = {"a%d" % i: shape[i + 1] for i in range(len(shape) - 1)}
        return v.rearrange("p (%s) -> p %s" % (names, names), **kw)

    bank_rr = [0]
    def bank():
        b = PS[bank_rr[0] % 8]
        bank_rr[0] += 1
        return b

    evac_rr = [0]
    def evac_eng():
        evac_rr[0] += 1
        return "scalar" if evac_rr[0] % 2 else "vector"

    wrr = [0]
    def wload(W, l, row0, nk, col0, ncols):
        slot = WS[wrr[0] % 4]
        wrr[0] += 1
        view = slot[:, 0:nk * ncols].rearrange("p (k n) -> p k n", n=ncols)
        src = W[l, row0:row0 + nk * 128, col0:col0 + ncols].rearrange("(k p) n -> p k n", p=128)
        P.dma(view, src, eng="gpsimd")
        return view

    P.dma(identf.all(), c_ident.all())
    P.copy(identb.all(), identf.all())
    ct = tv(0, [128, 128])
    P.dma(ct, c_caus.all())
    P.copy(caus4.all(), ct.map(lambda a: a.unsqueeze(1).broadcast_to([128, 4, 128])))
    ct2 = tv(128, [128, 128])
    P.dma(ct2, c_anti.all())
    P.copy(anti4.all(), ct2.map(lambda a: a.unsqueeze(1).broadcast_to([128, 4, 128])))
    P.dma(ebig.all(), c_ebig[:, 0:4096], eng="gpsimd")

    def transpose_to_T(dst, ntok_tiles, rows=128):
        for i in range(ntok_tiles):
            for k0 in range(0, 16, 4):
                pb = bank()
                for k in range(4):
                    P.op("tensor", "transpose", out=pb[:, k * 128:k * 128 + rows],
                         in_=ACC.p(i)[0:rows, i, (k0 + k) * 128:(k0 + k + 1) * 128], identity=identf[0:rows, 0:rows])
                P.copy(dst.p(*range(k0, k0 + 4))[:, k0:k0 + 4, i * 128:i * 128 + rows],
                       pb.all().rearrange("p (k t) -> p k t", k=4)[:, :, 0:rows], eng=evac_eng())

    def layer_norm(i, gb, ntok=128):
        st = tv(6000, [128, 4, 6]); mv = tv(6030, [128, 2]); rs = tv(6040, [128, 1])
        xr = ACC.p(i)[0:ntok, i, :]
        for q in range(4):
            P.op("vector", "bn_stats", out=st[0:ntok, q, :], in_=xr[:, q * 512:(q + 1) * 512])
        P.op("vector", "bn_aggr", out=mv[0:ntok, :], in_=st[0:ntok].rearrange("p a b -> p (a b)"))
        P.ts(rs[0:ntok], mv[0:ntok, 1:2], 1e-5, ALU.add)
        P.act(rs[0:ntok], rs[0:ntok], AF.Ln)
        P.act(rs[0:ntok], rs[0:ntok], AF.Exp, scale=-0.5)
        P.ts(xr, xr, mv[0:ntok, 0:1], ALU.subtract, rs[0:ntok, 0:1], ALU.mult)
        P.tt(xr, xr, gb[0:ntok, 0, :], ALU.mult)
        P.tt(xr, xr, gb[0:ntok, 1, :], ALU.add)

    def load_gb(l, g_d, b_d):
        gb = tv(0, [128, 2, D])
        P.dma(gb[:, 0, :], g_d[l:l + 1, :].map(lambda a: a.broadcast_to([128, D])))
        P.dma(gb[:, 1, :], b_d[l:l + 1, :].map(lambda a: a.broadcast_to([128, D])))
        return gb

    def layer_setup(l):
        P.dma(sc[:, 0:4], ssm_d[l].rearrange("(k p) -> p k", p=128), allow_slow_non_contiguous=True)
        P.dma(sc[:, 4:8], b_glu[l].rearrange("(k p) -> p k", p=128), allow_slow_non_contiguous=True)
        P.ts(sc[:, 4:8], sc[:, 4:8], -1.0, ALU.mult)
        for j in range(3):
            P.dma(sc[:, 8 + 4 * j:12 + 4 * j], conv_w[l, j].rearrange("(k p) -> p k", p=128), allow_slow_non_contiguous=True)
        P.dma(sc[:, 20:24], conv_b[l].rearrange("(k p) -> p k", p=128), allow_slow_non_contiguous=True)
        P.dma(sc[:, 24:28], pool_sc[l].rearrange("(k p) -> p k", p=128), allow_slow_non_contiguous=True)
        P.dma(wglu.all(), w_glu[l].rearrange("(k p) n -> p k n", p=128), eng="gpsimd")
        P.dma(poolw.all(), pool_w[l].rearrange("g c d -> c g d"), eng="gpsimd")
        for two in range(2):
            P.dma(wckT[two * 64:(two + 1) * 64, :], wck[l].rearrange("k d -> d k"), allow_slow_non_contiguous=True)
            P.dma(wcvT[two * 64:(two + 1) * 64, :], wcv[l].rearrange("k d -> d k"), allow_slow_non_contiguous=True)
        P.memset(hst.all(), 0.0); P.memset(convc.all(), 0.0); P.memset(poolc.all(), 0.0)
        P.memset(kcT.all(), 0.0); P.memset(vcT.all(), 0.0)
        P.memset(vc1.all(), 1.0); P.memset(Vsel1.all(), 1.0); P.memset(Vwin1.all(), 1.0)
        s = lambda o, n=16: tv(o, [128, n])
        are, aim, dtt, th, cs, sn, t1, t2, abr, abi, den, fre, fim = [s(16 * i) for i in range(13)]
        for two in range(2):
            pr = slice(two * 64, (two + 1) * 64)
            P.dma(are[pr, :], a_re[l].rearrange("(m two) n -> two n m", two=2)[two], allow_slow_non_contiguous=True)
            P.dma(aim[pr, :], a_im[l].rearrange("(m two) n -> two n m", two=2)[two], allow_slow_non_contiguous=True)
            P.dma(dtt[pr, :], ldt[l:l + 1, :].rearrange("o (m two) -> o two m", two=2)[:, two, :].map(lambda a: a.broadcast_to([64, 16])), allow_slow_non_contiguous=True)
        P.act(dtt, dtt, AF.Exp)
        P.tt(t1, dtt, are, ALU.mult)
        P.act(mag.all(), t1, AF.Exp)
        P.tt(th, dtt, aim, ALU.mult)
        P.act(sn, th, AF.Sin, scale=1.0 / 16)
        P.ts(t2, th, 1.0 / 16, ALU.mult, math.pi / 2, ALU.add)
        P.act(cs, t2, AF.Sin)
        for _ in range(4):
            P.tt(t1, cs, cs, ALU.mult); P.tt(t2, sn, sn, ALU.mult)
            P.tt(sn, sn, cs, ALU.mult); P.ts(sn, sn, 2.0, ALU.mult)
            P.tt(cs, t1, t2, ALU.subtract)
        P.tt(abr, mag.all(), cs, ALU.mult); P.tt(abi, mag.all(), sn, ALU.mult)
        P.tt(t1, are, are, ALU.mult); P.tt(t2, aim, aim, ALU.mult); P.tt(den, t1, t2, ALU.add)
        P.op("vector", "reciprocal", out=den, in_=den)
        nr = t1
        P.ts(nr, abr, -1.0, ALU.add)
        P.tt(fre, nr, are, ALU.mult); P.tt(t2, abi, aim, ALU.mult); P.tt(fre, fre, t2, ALU.add); P.tt(fre, fre, den, ALU.mult)
        P.tt(fim, abi, are, ALU.mult); P.tt(t2, nr, aim, ALU.mult); P.tt(fim, fim, t2, ALU.subtract); P.tt(fim, fim, den, ALU.mult)
        bre = tv(256, [128, 16, 16]); bim = tv(512, [128, 16, 16]); bbr = tv(768, [128, 16, 16]); bbi = tv(1024, [128, 16, 16]); tb = tv(1280, [128, 16, 16])
        cre = tv(1536, [128, 16, 16]); cim = tv(1792, [128, 16, 16])
        for two in range(2):
            pr = slice(two * 64, (two + 1) * 64)
            P.dma(bre[pr], b_re[l].rearrange("(m two) n c -> two n m c", two=2)[two])
            P.dma(bim[pr], b_im[l].rearrange("(m two) n c -> two n m c", two=2)[two])
            for m in range(16):
                P.dma(cre[pr, m, :], c_re[l, 2 * m + two].rearrange("c n -> n c"), allow_slow_non_contiguous=True)
                P.dma(cim[pr, m, :], c_im[l, 2 * m + two].rearrange("c n -> n c"), allow_slow_non_contiguous=True)
        bc = lambda v: v.map(lambda a: a.unsqueeze(2).broadcast_to([128, 16, 16]))
        P.tt(bbr, bre, bc(fre), ALU.mult); P.tt(tb, bim, bc(fim), ALU.mult); P.tt(bbr, bbr, tb, ALU.subtract)
        P.tt(bbi, bim, bc(fre), ALU.mult); P.tt(tb, bre, bc(fim), ALU.mult); P.tt(bbi, bbi, tb, ALU.add)
        P.ts(cim, cim, -1.0, ALU.mult)
        Z = ACC.p(0, 1)[:, 0:2, :].rearrange("p a (m r c) -> p (a m) r c", r=2, c=128)
        Cz = ACC.p(2, 3)[:, 2:4, :].rearrange("p a (m r c) -> p (a m) r c", r=2, c=128)
        P.memset(ACC.p(0, 1)[:, 0:2, :], 0.0); P.memset(ACC.p(2, 3)[:, 2:4, :], 0.0)
        for ri, (srcB, srcC) in enumerate(((bbr, cre), (bbi, cim))):
            for q4 in range(4):
                for two in range(2):
                    pr = slice(two * 64, (two + 1) * 64)
                    co = q4 * 32 + two * 16
                    P.copy(Z[pr, q4::4, ri, co:co + 16], srcB[pr, q4::4, :])
                    P.copy(Cz[pr, q4::4, ri, co:co + 16], srcC[pr, q4::4, :], eng="scalar")
        P.dma(ctd[l], Cz)
        for m in range(16):
            pb = bank()
            for ri in range(2):
                P.op("tensor", "transpose", out=pb[:, ri * 128:(ri + 1) * 128], in_=Z[:, m, ri, :], identity=identf.all())
            P.copy(BT[:, m, :, :], pb[:, 0:256].rearrange("p (r c) -> p r c", r=2), eng=evac_eng())
        hview = HT.all().rearrange("p a n -> p (a n)").bitcast(F32)
        Er = hview[:, 0:16 * LSEG].rearrange("p (m t) -> p m t", t=LSEG)
        xv = XT.all().rearrange("p a n -> p (a n)").bitcast(F32)
        Ei = xv[:, 0:16 * LSEG].rearrange("p (m t) -> p m t", t=LSEG)
        ta = hview[:, 4096:6144].rearrange("p (m t) -> p m t", t=128); tb2 = tv(2200, [128, 16, 128])
        pr_, pi_ = tv(2048, [128, 16]), tv(2064, [128, 16])
        q1, q2 = tv(2080, [128, 16]), tv(2096, [128, 16])
        P.copy(Er[:, :, 0], cs); P.copy(Ei[:, :, 0], sn)
        P.copy(pr_, cs); P.copy(pi_, sn)
        n = 1
        while n < LSEG:
            bcn = lambda v: v.map(lambda a: a.unsqueeze(2).broadcast_to([128, 16, n]))
            P.tt(ta[:, :, 0:n], Er[:, :, 0:n], bcn(pr_), ALU.mult)
            P.tt(tb2[:, :, 0:n], Ei[:, :, 0:n], bcn(pi_), ALU.mult)
            P.tt(Er[:, :, n:2 * n], ta[:, :, 0:n], tb2[:, :, 0:n], ALU.subtract)
            P.tt(ta[:, :, 0:n], Er[:, :, 0:n], bcn(pi_), ALU.mult)
            P.tt(tb2[:, :, 0:n], Ei[:, :, 0:n], bcn(pr_), ALU.mult)
            P.tt(Ei[:, :, n:2 * n], ta[:, :, 0:n], tb2[:, :, 0:n], ALU.add)
            P.tt(q1, pr_, pr_, ALU.mult); P.tt(q2, pi_, pi_, ALU.mult)
            P.tt(pi_, pr_, pi_, ALU.mult); P.ts(pi_, pi_, 2.0, ALU.mult)
            P.tt(pr_, q1, q2, ALU.subtract)
            n *= 2
        P.dma(tabd[l][:, :, 0, :], Er); P.dma(tabd[l][:, :, 1, :], Ei)

    def s5_mixer(l, nseq, seqlen, chained, state_tile, ntok):
        ysb = tv(3072, [128, ntok])
        rr = [0]
        for kcc in range(4):
            ctl = tv(0, [128, 4, 2, 128])
            P.dma(ctl, ctd[l][:, kcc * 4:(kcc + 1) * 4])
            tab4 = tv(1024, [128, 4, 2, seqlen])
            for mm_ in range(4):
                P.dma(tab4[:, mm_], tabd[l][:, kcc * 4 + mm_, :, 0:seqlen])
            uT = HT.p(kcc)[:, kcc, 0:ntok]
            for sq in range(nseq):
                cols = slice(sq * seqlen, (sq + 1) * seqlen)
                ybank = PS[6 + sq % 2]
                for mm_ in range(4):
                    m = kcc * 4 + mm_
                    ErL = tab4[:, mm_, 0:1, :].map(lambda a: a.broadcast_to([128, 2, seqlen]))
                    EiL = tab4[:, mm_, 1:2, :].map(lambda a: a.broadcast_to([128, 2, seqlen]))
                    pa = PS[(2 * rr[0]) % 6]; pbk = PS[(2 * rr[0] + 1) % 6]
                    rr[0] += 1
                    L2 = 2 * seqlen
                    P.mm(pa[:, 0:seqlen], BT[:, m, 0, :], uT[:, cols]); P.mm(pa[:, seqlen:L2], BT[:, m, 1, :], uT[:, cols])
                    P.mm(pbk[:, 0:seqlen], BT[:, m, 1, :], uT[:, cols]); P.mm(pbk[:, seqlen:L2], BT[:, m, 0, :], uT[:, cols])
                    t1 = tv(3584, [128, 2, seqlen]); t2 = tv(4096, [128, 2, seqlen])
                    G3 = tv(4608, [128, 3, seqlen]); hh = tv(5376, [128, 2, seqlen])
                    P.tt(t1, pa[:, 0:L2].rearrange("p (r t) -> p r t", r=2), ErL, ALU.mult)
                    P.tt(t2, pbk[:, 0:L2].rearrange("p (r t) -> p r t", r=2), EiL, ALU.mult)
                    P.tt(t1[:, 0, :], t1[:, 0, :], t2[:, 0, :], ALU.add)
                    P.tt(t1[:, 1, :], t1[:, 1, :], t2[:, 1, :], ALU.subtract)
                    si = (sq if not chained else 0)
                    magb = mag[:, m:m + 1].map(lambda a: a.broadcast_to([128, seqlen]))
                    for r in range(2):
                        P.op("vector", "tensor_tensor_scan", out=G3[:, r, :], data0=magb, data1=t1[:, r, :],
                             initial=state_tile[:, m, si, r:r + 1], op0=ALU.mult, op1=ALU.add)
                    P.copy(G3[:, 2, :], G3[:, 0, :], eng="scalar")
                    P.tt(t1, G3[:, 0:2, :], ErL, ALU.mult)
                    P.tt(t2, G3[:, 1:3, :], EiL, ALU.mult)
                    P.tt(hh[:, 0, :], t1[:, 0, :], t2[:, 0, :], ALU.subtract)
                    P.tt(hh[:, 1, :], t1[:, 1, :], t2[:, 1, :], ALU.add)
                    P.copy(state_tile[:, m, si, :], hh[:, :, seqlen - 1], eng="scalar")
                    for r in range(2):
                        P.mm(ybank[:, 0:seqlen], ctl[:, mm_, r, :], hh[:, r, :], start=(mm_ == 0 and r == 0), stop=(mm_ == 3 and r == 1))
                P.stt(ysb[:, cols], uT[:, cols], sc[:, kcc:kcc + 1], ybank[:, 0:seqlen], ALU.mult, ALU.add)
            ys = ysb
            g1 = tv(3584, [128, ntok]); g2 = tv(4096, [128, ntok])
            P.tt(g1, ys, ys, ALU.mult)
            P.ts(g1, g1, 0.044715, ALU.mult, 1.0, ALU.add)
            P.tt(g1, g1, ys, ALU.mult)
            P.act(g2, g1, AF.Exp, scale=-GC)
            P.ts(g2, g2, 1.0, ALU.add)
            P.op("vector", "reciprocal", out=g2, in_=g2)
            P.tt(HT.p(kcc)[:, kcc, 0:ntok], ys, g2, ALU.mult)
        zb = [PS[0], PS[1], PS[2], PS[3]]
        for oc in range(4):
            for kcc in range(4):
                P.mm(zb[oc][:, 0:ntok], wglu[:, kcc, oc * 128:(oc + 1) * 128], HT.p(kcc)[:, kcc, 0:ntok], start=(kcc == 0), stop=(kcc == 3))
        ya = tv(1024, [128, 4, ntok], BF16)
        for oc in range(4):
            e = tv(2048 + (oc % 2) * 512, [128, ntok])
            P.act(e, zb[oc][:, 0:ntok], AF.Exp, scale=-1.0, bias=sc[:, 4 + oc:5 + oc])
            P.ts(e, e, 1.0, ALU.add)
            P.op("vector", "reciprocal", out=e, in_=e)
            P.tt(ya[:, oc, :], HT.p(oc)[:, oc, 0:ntok], e, ALU.mult)
        for oc in range(4):
            P.copy(HT.p(oc)[:, oc, 0:ntok], ya[:, oc, :], eng="scalar")

    def conv_mixer(nseq, T, carry, ntok):
        W = T + 2
        for kcc in range(4):
            ext = tv(0, [128, nseq, W])
            a = tv(1100 + (kcc % 2) * 600, [128, nseq, T])
            gb = HT.p(4 + kcc)[:, 4 + kcc, 0:ntok].rearrange("p (s t) -> p s t", s=nseq)
            gc = HT.p(8 + kcc)[:, 8 + kcc, 0:ntok].rearrange("p (s t) -> p s t", s=nseq)
            vv = HT.p(12 + kcc)[:, 12 + kcc, 0:ntok].rearrange("p (s t) -> p s t", s=nseq)
            P.copy(ext[:, :, 0:2], carry[:, kcc, :, :], eng="scalar")
            P.tt(ext[:, :, 2:W], gc, vv, ALU.mult)
            P.copy(carry[:, kcc, :, :], ext[:, :, T:W], eng="scalar")
            P.ts(a, ext[:, :, 0:T], sc[:, 8 + kcc:9 + kcc], ALU.mult, sc[:, 20 + kcc:21 + kcc], ALU.add)
            P.stt(a, ext[:, :, 1:T + 1], sc[:, 12 + kcc:13 + kcc], a, ALU.mult, ALU.add)
            P.stt(a, ext[:, :, 2:T + 2], sc[:, 16 + kcc:17 + kcc], a, ALU.mult, ALU.add)
            P.tt(gb, gb, a, ALU.mult)

    def pool_mixer(nseq, T, carry, ntok, first_chunk):
        W = T + 15
        wins_ = (2, 4, 8, 16)
        for kcc in range(4):
            w = wins_[kcc]
            e0 = tv(0, [128, nseq, W]); e1 = tv(2200, [128, nseq, W])
            up = HT.p(16 + kcc)[:, 16 + kcc, 0:ntok].rearrange("p (s t) -> p s t", s=nseq)
            P.copy(e0[:, :, 0:15], carry[:, kcc, :, :], eng="scalar")
            P.copy(e0[:, :, 15:W], up)
            P.copy(carry[:, kcc, :, :], e0[:, :, T:W], eng="scalar")
            src, dst = e0, e1
            sh = 1
            while sh < w:
                P.tt(dst[:, :, sh:W], src[:, :, sh:W], src[:, :, 0:W - sh], ALU.add)
                if sh > 0:
                    P.copy(dst[:, :, 0:sh], src[:, :, 0:sh], eng="scalar")
                src, dst = dst, src
                sh *= 2
            pl = tv(4400, [128, nseq, T], BF16)
            P.stt(pl, src[:, :, 15:W], 1.0 / w, up, ALU.mult, ALU.subtract)
            if first_chunk:
                rc = tv(5000, [128, 16]); t16 = tv(5020, [128, 16])
                P.dma(rc, c_rcnt0[:, kcc, :])
                P.tt(t16, src[:, 0, 15:31], rc, ALU.mult)
                P.tt(pl[:, 0, 0:16], t16, up[:, 0, 0:16], ALU.subtract)
            pb = bank()
            P.mm(pb[:, 0:ntok], poolw[:, kcc, :], pl.rearrange("p s t -> p (s t)"))
            P.ts(HT.p(16 + kcc)[:, 16 + kcc, 0:ntok], pb[:, 0:ntok], sc[:, 24 + kcc:25 + kcc], ALU.mult)

    def nsa_prompt(l, c):
        ydT_done = []
        for i in range(NT):
            T = 4 * c + i
            tm = tv(0, [128, 3, 64]); cbT = tv(192, [64, 128])
            P.dma(tm, c_tmask[T]); P.dma(cbT, c_cmpbT[T])
            cbT4 = tv(320, [128, 4, 128], BF16)
            P.memset(cbT4[64:128], NEG)
            P.copy(cbT4[0:64], cbT.map(lambda a: a.unsqueeze(1).broadcast_to([64, 4, 128])))
            yd = tv(600, [128, 8, 64])
            qc = tv(4500, [128, 4, 128], BF16)
            P.copy(qc, HT.p(20, 21, 22, 23)[:, 20:24, i * 128:(i + 1) * 128], eng="scalar")
            for g in range(2):
                gp = slice(g * 64, (g + 1) * 64)
                qg = qc[gp].rearrange("p r t -> p (r t)")
                sb = PS[5]
                for r in range(4):
                    P.mm(sb[:, r * 64:(r + 1) * 64], HT.p(20 + r)[gp, 20 + r, i * 128:(i + 1) * 128], kcT[gp, :])
                s_ = tv(1200, [128, 4, 64]); sm = tv(1460, [128, 4]); imp = tv(1470, [128, 64]); m8 = tv(1540, [128, 16]); sc2 = tv(1560, [128, 64])
                P.stt(s_, sb[:, 0:256].rearrange("p (r j) -> p r j", r=4), 0.125, tm[:, 0:1, :].map(lambda a: a.broadcast_to([128, 4, 64])), ALU.mult, ALU.add)
                P.act(s_, s_, AF.Exp)
                P.op("vector", "tensor_reduce", out=sm, in_=s_, axis=AX.X, op=ALU.add)
                P.ts(sm, sm, 1e-30, ALU.max)
                P.op("vector", "reciprocal", out=sm, in_=sm)
                P.tt(s_, s_, sm.map(lambda a: a.unsqueeze(2).broadcast_to([128, 4, 64])), ALU.mult)
                P.op("vector", "tensor_reduce", out=imp, in_=s_.rearrange("p r j -> p j r"), axis=AX.X, op=ALU.add)
                P.tt(imp, imp, tm[:, 1, :], ALU.mult)
                P.tt(imp, imp, tm[:, 2, :], ALU.add)
                P.op("vector", "max", out=m8[:, 0:8], in_=imp)
                P.op("vector", "match_replace", out=sc2, in_to_replace=m8[:, 0:8], in_values=imp, imm_value=-1e30)
                P.op("vector", "max", out=m8[:, 8:16], in_=sc2)
                P.ts(sc2, imp, m8[:, 15:16], ALU.is_lt, NEG, ALU.mult)
                tb_ = PS[5]
                P.op("tensor", "transpose", out=tb_[0:64, 256:384], in_=sc2, identity=identf.all())
                sbT4 = tv(1700, [128, 4, 128], BF16)
                P.memset(sbT4[64:128], 0.0)
                P.copy(sbT4[0:64], tb_[0:64, 256:384].map(lambda a: a.unsqueeze(1).broadcast_to([64, 4, 128])))
                qz = tv(4760, [128, 512], BF16)
                P.memset(qz, 0.0)
                P.copy(qz[gp], qg, eng="scalar")
                qfull = qc.rearrange("p r t -> p (r t)")
                if NSA_LEVEL < 2:
                    continue
                Osb = [tv(5200 + bi_ * 260, [128, 4, 65]) for bi_ in range(3)]
                def branch(bi_, items):
                    n = len(items)
                    for idx, (lhs_list, vrhs, kp) in enumerate(items):
                        stb = PS[idx % 2]
                        for k_, (lt, rh) in enumerate(lhs_list):
                            P.mm(stb[0:kp, :], lt, rh, start=(k_ == 0), stop=(k_ == len(lhs_list) - 1))
                        pt = tv(2300 + (idx % 3) * 256, [128, 512], BF16)
                        P.act(pt[0:kp, :], stb[0:kp, :], AF.Exp, scale=0.125)
                        for r in range(4):
                            P.mm(PS[2 + r][:, 0:65], pt[0:kp, r * 128:(r + 1) * 128], vrhs, start=(idx == 0), stop=(idx == n - 1))
                    for r in range(4):
                        P.copy(Osb[bi_][:, r, :], PS[2 + r][:, 0:65], eng="scalar")
                q2 = qg
                branch(0, [([(kcTz[:, g, :], qfull), (identb.all(), cbT4.rearrange('p r t -> p (r t)'))], vc1[:, g, 0:65], 128)])
                if NSA_LEVEL < 3:
                    continue
                items = []
                for kc_ in range(T + 1):
                    ll = [(KselT.p(kc_)[:, kc_ * 128:(kc_ + 1) * 128], qz), (ebig[:, kc_ * 128:(kc_ + 1) * 128], sbT4.rearrange('p r t -> p (r t)'))]
                    if kc_ == T:
                        ll.append((identb.all(), caus4.all().rearrange('p r t -> p (r t)')))
                    items.append((ll, Vsel1.p(kc_)[:, kc_, g, 0:65], 128))
                branch(1, items)
                if NSA_LEVEL < 4:
                    continue
                items = []
                for kc_ in range(max(0, T - 4), T + 1):
                    ll = [(KwinT.p(kc_ % 8)[:, kc_ % 8, :], qz)]
                    if kc_ == T:
                        ll.append((identb.all(), caus4.all().rearrange('p r t -> p (r t)')))
                    elif kc_ == T - 4:
                        ll.append((identb.all(), anti4.all().rearrange('p r t -> p (r t)')))
                    items.append((ll, Vwin1.p(kc_ % 8)[:, kc_ % 8, g, 0:65], 128))
                branch(2, items)
                if NSA_LEVEL < 5:
                    continue
                for bi in range(3):
                    O3 = Osb[bi]
                    dn = tv(3100, [128, 4]); tmpo = tv(3110, [128, 4, 64])
                    P.ts(dn, O3[:, :, 64], 1e-30, ALU.max)
                    P.op("vector", "reciprocal", out=dn, in_=dn)
                    P.tt(dn, dn, gate[:, i, :].rearrange("p (h b) -> p h b", b=3)[:, 4 * g:4 * g + 4, bi], ALU.mult)
                    dst = yd[:, 4 * g:4 * g + 4, :]
                    if bi == 0:
                        P.tt(dst, O3[:, :, 0:64], dn.map(lambda a: a.unsqueeze(2).broadcast_to([128, 4, 64])), ALU.mult)
                    else:
                        P.tt(tmpo, O3[:, :, 0:64], dn.map(lambda a: a.unsqueeze(2).broadcast_to([128, 4, 64])), ALU.mult)
                        P.tt(dst, dst, tmpo, ALU.add)
            if NSA_LEVEL < 5:
                continue
            pb = PS[7]
            ydf = yd.rearrange("p h d -> p (h d)")
            for k in range(4):
                P.op("tensor", "transpose", out=pb[:, k * 128:(k + 1) * 128], in_=ydf[:, k * 128:(k + 1) * 128], identity=identf.all())
            ydT_done.append((i, pb))
            P.copy(tv(3400 + i * 256, [128, 4, 128], BF16), pb.all().rearrange("p (k t) -> p k t", k=4), eng="scalar")
        if NSA_LEVEL < 5:
            return
        for i in range(NT):
            P.copy(HT.p(20, 21, 22, 23)[:, 20:24, i * 128:(i + 1) * 128], tv(3400 + i * 256, [128, 4, 128], BF16))

    def prompt_chunk(l, c):
        t0 = c * NCH
        src = x_p if l == 0 else x1d
        P.dma(ACC.all(), src[t0:t0 + NCH, :].rearrange("(i p) f -> p i f", p=128))
        transpose_to_T(XT, NT)
        kvtok = tv(2900, [128, NT, 792])
        def fm_block(col0, chunk_cols, dsts, tok=False, tok_cols=None, tok_off=0):
            banks = [bank() for _ in chunk_cols]
            tbanks = [bank() for _ in range(NT)] if tok else []
            ncols = tok_cols if tok else 512
            for half in range(2):
                wv = wload(w_in, l, half * 1024, 8, col0, ncols)
                for ci, mk in enumerate(chunk_cols):
                    for k in range(8):
                        kk = half * 8 + k
                        P.mm(banks[ci].all(), mk(wv, k), XT.p(kk)[:, kk, :], start=(kk == 0), stop=(kk == 15))
                for i in range(len(tbanks)):
                    for k in range(8):
                        kk = half * 8 + k
                        P.mm(tbanks[i][:, 0:ncols], XT.p(kk)[:, kk, i * 128:(i + 1) * 128], wv[:, k, :], start=(kk == 0), stop=(kk == 15))
            for ci, dv in enumerate(dsts):
                P.copy(dv, banks[ci].all(), eng=evac_eng())
            for i in range(len(tbanks)):
                P.copy(kvtok[:, i, tok_off:tok_off + ncols], tbanks[i][:, 0:ncols], eng=evac_eng())
        std = lambda j: (lambda wv, k: wv[:, k, j * 128:(j + 1) * 128])
        for blk in range(5):
            fm_block(blk * 512, [std(j) for j in range(4)], [HT.p(blk * 4 + j)[:, blk * 4 + j, :] for j in range(4)])
        fm_block(2560, [std(r) for r in range(4)], [HT.p(20 + r)[:, 20 + r, :] for r in range(4)])
        Tt = [4 * c + i for i in range(NT)]
        fm_block(3072, [std(0), std(1), std(2)],
                 [HT.p(24)[:, 24, :], HT.p(25)[:, 25, :], KselT.p(*Tt)[:, t0:t0 + NCH]], tok=True, tok_cols=512, tok_off=0)
        ws_ = (4 * c) % 8
        fm_block(3584, [std(0)], [KwinT.p(*range(ws_, ws_ + 4))[:, ws_:ws_ + 4, :].rearrange("p a t -> p (a t)")],
                 tok=True, tok_cols=280, tok_off=512)
        P.dma(kvp[l, t0:t0 + NCH, :].rearrange("(i p) f -> p i f", p=128), kvtok[:, :, 0:512], is_output=True)
        if c == n_chunks - 1:
            P.dma(winp[l].rearrange("(i p) f -> p i f", p=128), kvtok[:, :, 512:768], is_output=True)
        P.copy(Vsel1.p(*Tt)[:, 4 * c:4 * c + 4, :, 0:64], kvtok[:, :, 384:512].rearrange("p i (g d) -> p i g d", g=2))
        P.copy(Vwin1.p(*range(ws_, ws_ + 4))[:, ws_:ws_ + 4, :, 0:64], kvtok[:, :, 640:768].rearrange("p i (g d) -> p i g d", g=2))
        P.act(gate.all(), kvtok[:, :, 768:792], AF.Exp, scale=-1.0)
        P.ts(gate.all(), gate.all(), 1.0, ALU.add)
        P.op("vector", "reciprocal", out=gate.all(), in_=gate.all())
        for srcc, wT, dstT in ((24, wckT, kcT), (25, wcvT, vcT)):
            tc_ = tv(0, [128, 8, 64])
            P.tt(tc_, HT.p(srcc)[:, srcc, :].rearrange("p (b k) -> p b k", k=64), wT.all().map(lambda a: a.unsqueeze(1).broadcast_to([128, 8, 64])), ALU.mult)
            P.op("vector", "tensor_reduce", out=dstT[:, 8 * c:8 * c + 8], in_=tc_, axis=AX.X, op=ALU.add)
        pbv = PS[6]
        pv16 = pbv.all().bitcast(BF16)
        P.op("tensor", "transpose", out=pv16[0:64, 0:128], in_=vcT.all(), identity=identb.all())
        P.copy(vc1[0:64, :, 0:64], pv16[0:64, 0:128].rearrange("p (g d) -> p g d", g=2))
        P.memset(kcTz.all(), 0.0)
        for g_ in range(2):
            P.copy(kcTz[g_ * 64:(g_ + 1) * 64, g_, 0:64], kcT[g_ * 64:(g_ + 1) * 64, :])
        if "s5" in parts:
            s5_mixer(l, NCH // LSEG, LSEG, True, hst.all().rearrange("p m (s r) -> p m s r", s=1), NCH)
        if "conv" in parts:
            conv_mixer(1, NCH, convc.all().rearrange("p k (s j) -> p k s j", s=1), NCH)
        if "pool" in parts:
            pool_mixer(1, NCH, poolc.all().rearrange("p k (s j) -> p k s j", s=1), NCH, first_chunk=(c == 0))
        if "nsa" in parts:
            nsa_prompt(l, c)
        if c == n_chunks - 1:
            P.dma(sre_p[l].rearrange("(m two) n -> two n m", two=2)[0], hst[0:64, :, 0], is_output=True, allow_slow_non_contiguous=True)
            P.dma(sre_p[l].rearrange("(m two) n -> two n m", two=2)[1], hst[64:128, :, 0], is_output=True, allow_slow_non_contiguous=True)
            P.dma(sim_p[l].rearrange("(m two) n -> two n m", two=2)[0], hst[0:64, :, 1], is_output=True, allow_slow_non_contiguous=True)
            P.dma(sim_p[l].rearrange("(m two) n -> two n m", two=2)[1], hst[64:128, :, 1], is_output=True, allow_slow_non_contiguous=True)
            for j in range(2):
                P.dma(conv_p[l, j].rearrange("(k p) -> p k", p=128), convc[:, :, j], is_output=True, allow_slow_non_contiguous=True)
            for j in range(15):
                P.dma(pool_p[l, j].rearrange("(k p) -> p k", p=128), poolc[:, :, j], is_output=True, allow_slow_non_contiguous=True)
        if DEBUG:
            P.dma(d["dbg"][c][:, 0:26], HT.all()[:, 0:26, :], eng="gpsimd", is_output=True)
        if "ffn" not in parts:
            return
        dense_tail(l, NT, 128, NCH)
        dst = x1d if l < n_layers - 1 else y_p
        P.dma(dst[t0:t0 + NCH, :].rearrange("(i p) f -> p i f", p=128), ACC.all(), is_output=(l == n_layers - 1))

    def dense_tail(l, ntl, rows, ntok):
        ycat = [0, 1, 2, 3, 4, 5, 6, 7, 16, 17, 18, 19, 20, 21, 22, 23]
        tsz = lambda i: slice(i * 128, i * 128 + rows)
        for j in range(4):
            banks = [bank() for _ in range(ntl)]
            for half in range(2):
                wv = wload(w_out, l, half * 1024, 8, j * 512, 512)
                for i in range(ntl):
                    for k in range(8):
                        kk = half * 8 + k
                        P.mm(banks[i][0:rows, :], HT.p(ycat[kk])[:, ycat[kk], tsz(i)], wv[:, k, :], start=(kk == 0), stop=(kk == 15))
            for i in range(ntl):
                av = ACC.p(i)[0:rows, i, j * 512:(j + 1) * 512]
                P.stt(av, av, ALPHA_, banks[i][0:rows, :], ALU.mult, ALU.add)
        gb = load_gb(l, ln1g, ln1b)
        for i in range(ntl):
            layer_norm(i, gb, rows)
        transpose_to_T(XT, ntl, rows)
        for i in range(ntl):
            P.ts(ACC.p(i)[0:rows, i, :], ACC.p(i)[0:rows, i, :], ALPHA_, ALU.mult)
        for fg in range(16):
            ub = [bank() for _ in range(4)]
            for half in range(2):
                wv = wload(w_up, l, half * 1024, 8, fg * 512, 512)
                for j in range(4):
                    for k in range(8):
                        kk = half * 8 + k
                        P.mm(ub[j][:, 0:ntok], wv[:, k, j * 128:(j + 1) * 128], XT.p(kk)[:, kk, 0:ntok], start=(kk == 0), stop=(kk == 15))
            hid = tv(4096 + (fg % 2) * 1024, [128, 4, NCH], BF16)
            for j in range(4):
                r_ = tv((j % 2) * 256, [128, NCH], BF16)
                P.act(r_[:, 0:ntok], ub[j][:, 0:ntok], AF.Relu)
                P.tt(hid[:, j, 0:ntok], r_[:, 0:ntok], r_[:, 0:ntok], ALU.mult)
            for half in range(2):
                wv = wload(w_down, l, fg * 512, 4, half * 1024, 1024)
                for i in range(ntl):
                    for jj in range(2):
                        pb = bank()
                        for k in range(4):
                            P.mm(pb[0:rows, :], hid[:, k, tsz(i)], wv[:, k, jj * 512:(jj + 1) * 512], start=(k == 0), stop=(k == 3))
                        av = ACC.p(i)[0:rows, i, (half * 2 + jj) * 512:(half * 2 + jj + 1) * 512]
                        P.tt(av, av, pb[0:rows, :], ALU.add)
        gb = load_gb(l, ln2g, ln2b)
        for i in range(ntl):
            layer_norm(i, gb, rows)

    sst = P.sbuf("sst", [128, 16, NB, 2], F32)
    convc_s = P.sbuf("convc_s", [128, 4, NB, 2], F32); poolc_s = P.sbuf("poolc_s", [128, 4, NB, 15], F32)
    xs1d = P.dram("xs1d", [NS, D], F32, IN)
    c_iota = d["c_iota"]; c_ibig = d["c_ibig"]; c_nb = d["c_nb"]; c_wb0 = d["c_wb0"]; c_selb = d["c_selb"]; c_rm = d["c_rm"]
    ibig = P.sbuf("ibig", [128, 254], BF16); iota = P.sbuf("iota", [128, 1], F32)
    P.dma(ibig.all(), c_ibig.all(), eng="gpsimd"); P.dma(iota.all(), c_iota.all())

    def nsa_sample(l, kvtok):
        ebs = XT.all().rearrange("p a n -> p (a n)")
        P.dma(ebs, c_ebig.all(), eng="gpsimd")
        wtok = tv(4888, [128, 256])
        for half in range(2):
            pr = slice(half * 64, half * 64 + 64)
            for g in range(2):
                P.dma(wtok[pr, g * 64:(g + 1) * 64], wck[l]); P.dma(wtok[pr, 128 + g * 64:128 + (g + 1) * 64], wcv[l])
        qcs = tv(5144, [128, NB, 16], BF16)
        P.copy(qcs.rearrange("p b (r t) -> p r b t", r=4), HT.p(20, 21, 22, 23)[:, 20:24, 0:NS].rearrange("p r (b t) -> p r b t", t=4))
        qzs = [tv(5816, [128, NB, 16], BF16), tv(5944, [128, NB, 16], BF16)]
        for g_ in range(2):
            P.memset(qzs[g_], 0.0)
            P.copy(qzs[g_][g_ * 64:(g_ + 1) * 64], qcs[g_ * 64:(g_ + 1) * 64], eng="scalar")
        nbias = tv(5272, [128, NB, 16], BF16); wb0 = tv(5400, [128, 16], BF16); Selb = tv(5552, [NS, NB, 16]); rm = tv(5808, [16, 6])
        P.memset(nbias, 0.0); P.dma(nbias[0:NS], c_nb.all(), eng="gpsimd"); P.dma(wb0, c_wb0.all(), eng="gpsimd"); P.dma(Selb, c_selb.all()); P.dma(rm, c_rm.all())
        Vnew1 = tv(5408, [128, 2, 2, 72], BF16)
        P.memset(Vnew1, 1.0)
        P.copy(Vnew1[0:NS, 0, :, 0:64], kvtok[0:NS, 0, 384:512].rearrange("p (g d) -> p g d", g=2))
        P.copy(Vnew1[0:NS, 1, :, 0:64], kvtok[0:NS, 0, 640:768].rearrange("p (g d) -> p g d", g=2))
        PTn = tv(4436, [128, 16], BF16)
        P.memset(PTn, 0.0)
        table = cache.all().rearrange("l r (h c) -> (l r h) c", h=2)
        for b in range(NB):
            ptb = tv(3168, [128, 64]).bitcast(I32); idf = tv(3232, [128, 64]); idxA = tv(3296, [128, 64]).bitcast(I32); idxB = tv(3360, [128, 64]).bitcast(I32)
            P.dma(ptb, ptab[b:b + 1, :].map(lambda a: a.broadcast_to([128, 64])))
            P.copy(idf, ptb)
            P.ts(idf, idf, 128.0, ALU.mult, iota[:, 0:1], ALU.add)
            if l > 0:
                P.ts(idf, idf, float(l * CACHE_ROWS), ALU.add)
            P.ts(idf, idf, 2.0, ALU.mult)
            P.copy(idxA, idf)
            P.ts(idf, idf, 1.0, ALU.add)
            P.copy(idxB, idf)
            def gather(dst, idx_col):
                def fn(e, o=dst.ap, i=table.ap, ia=idx_col.ap):
                    return e.indirect_dma_start(out=o, out_offset=None, in_=i, in_offset=bass.IndirectOffsetOnAxis(ap=ia, axis=0))
                P.dma_custom(fn, reads=[idx_col, table], writes=[dst])
            kvc = PS[7]
            for grp in range(16):
                pg = tv((grp % 2) * 1024, [128, 4, 256])
                for j in range(4):
                    gather(pg[:, j, :], idxA[:, grp * 4 + j:grp * 4 + j + 1])
                kw = tv(2048, [128, 4, 256], BF16)
                P.tt(kw, pg, wtok.map(lambda a: a.unsqueeze(1).broadcast_to([128, 4, 256])), ALU.mult)
                for j in range(4):
                    pgi = grp * 4 + j
                    P.mm(kvc[:, 0:256], ibig[:, 126 - 2 * pgi:254 - 2 * pgi], kw[:, j, :], start=(pgi == 0), stop=(pgi == 63))
            kc_tok = tv(3424, [128, 128], BF16); kcT_s = tv(3488, [128, 128], BF16); vc1_s = tv(3552, [128, 2, 72], BF16)
            P.copy(kc_tok, kvc[:, 0:128], eng="scalar")
            P.memset(vc1_s, 1.0)
            P.copy(vc1_s[:, :, 0:64], kvc[:, 128:256].rearrange("p (g d) -> p g d", g=2))
            tp = PS[6].all().bitcast(BF16)
            P.op("tensor", "transpose", out=tp[:, 0:128], in_=kc_tok, identity=identb.all())
            P.copy(kcT_s, tp[:, 0:128], eng="scalar")
            selbT = tv(4412, [128, 2, 16], BF16)
            Osb = tv(2048, [16, 3, 2, 65])
            for g in range(2):
                gp = slice(g * 64, (g + 1) * 64)
                qb = qcs[gp, b, :]
                sb = PS[5]
                for r in range(4):
                    P.mm(sb[0:4, r * 128:(r + 1) * 128], qcs[gp, b, r * 4:(r + 1) * 4], kcT_s[gp, :])
                s_ = tv(3624, [4, 4, 128]); sm = tv(4136, [4, 4]); imp = tv(4140, [4, 128]); m8 = tv(4268, [4, 16]); sc2 = tv(4284, [4, 128])
                P.act(s_, sb[0:4, :].rearrange("p (r j) -> p r j", r=4), AF.Exp, scale=0.125)
                P.op("vector", "tensor_reduce", out=sm, in_=s_, axis=AX.X, op=ALU.add)
                P.op("vector", "reciprocal", out=sm, in_=sm)
                P.tt(s_, s_, sm.map(lambda a: a.unsqueeze(2).broadcast_to([4, 4, 128])), ALU.mult)
                P.op("vector", "tensor_reduce", out=imp, in_=s_.rearrange("p r j -> p j r"), axis=AX.X, op=ALU.add)
                P.memset(imp[:, 0:1], 5.0); P.memset(imp[:, 127:128], 5.0)
                P.op("vector", "max", out=m8[:, 0:8], in_=imp)
                P.op("vector", "match_replace", out=sc2, in_to_replace=m8[:, 0:8], in_values=imp, imm_value=-1e30)
                P.op("vector", "max", out=m8[:, 8:16], in_=sc2)
                P.ts(sc2, imp, m8[:, 14:15], ALU.is_lt, NEG, ALU.mult)
                tq = PS[5]
                P.op("tensor", "transpose", out=tq[:, 508:512], in_=sc2, identity=identf[0:4, 0:4])
                P.copy(selbT[:, g, :].rearrange("p (r t) -> p r t", r=4), tq[:, 508:512].map(lambda a: a.unsqueeze(1).broadcast_to([128, 4, 4])))
                stc = PS[0]
                P.mm(stc[:, 0:16], kcT_s[gp, :], qb)
                PcT = tv(4428, [128, 16], BF16)
                P.act(PcT, stc[:, 0:16], AF.Exp, scale=0.125)
                oc_ = PS[2]
                P.mm(oc_[0:16, 0:65], PcT, vc1_s[:, g, 0:65])
                P.copy(Osb[:, 0, g, :], oc_[0:16, 0:65], eng="scalar")
            osel = [PS[3], PS[4]]
            V1g = tv(2816, [128, 4, 2, 72], BF16)
            P.memset(V1g, 1.0)
            for grp in range(16):
                pg = tv((grp % 2) * 1024, [128, 4, 256])
                for j in range(4):
                    gather(pg[:, j, :], idxB[:, grp * 4 + j:grp * 4 + j + 1])
                tb_ = PS[grp % 2]
                for j in range(4):
                    P.op("tensor", "transpose", out=tb_[:, j * 128:(j + 1) * 128], in_=pg[:, j, 0:128], identity=identf.all())
                KT = tv(2560, [128, 512], BF16)
                P.copy(KT, tb_.all(), eng="scalar")
                P.copy(V1g[:, :, :, 0:64], pg[:, :, 128:256].rearrange("p j (g d) -> p j g d", g=2))
                stb = PS[5]
                for j in range(4):
                    pgi = grp * 4 + j
                    for g in range(2):
                        gp = slice(g * 64, (g + 1) * 64)
                        oc = (j * 2 + g) * 16
                        P.mm(stb[:, oc:oc + 16], KT[:, j * 128:(j + 1) * 128], qzs[g][:, b, :], start=True, stop=False)
                        P.mm(stb[:, oc:oc + 16], ebs[:, pgi * 128:(pgi + 1) * 128], selbT[:, g, :], start=False, stop=True)
                PT = tv(3104, [128, 128], BF16)
                P.act(PT, stb[:, 0:128], AF.Exp, scale=0.125)
                for j in range(4):
                    for g in range(2):
                        oc = (j * 2 + g) * 16
                        P.mm(osel[g][0:16, 0:65], PT[:, oc:oc + 16], V1g[:, j, g, 0:65], start=(grp == 0 and j == 0), stop=False)
            owin = [PS[6], PS[7]]
            wbuf = tv(0, [128, 4, 256])
            P.dma(wbuf, cwin[l, b].rearrange("(i p) f -> p i f", p=128))
            tbw = PS[1]
            for i in range(4):
                P.op("tensor", "transpose", out=tbw[:, i * 128:(i + 1) * 128], in_=wbuf[:, i, 0:128], identity=identf.all())
            KTw = tv(2560, [128, 512], BF16)
            P.copy(KTw, tbw.all(), eng="scalar")
            P.copy(V1g[:, :, :, 0:64], wbuf[:, :, 128:256].rearrange("p j (g d) -> p j g d", g=2))
            stw = PS[5]
            for i in range(4):
                for g in range(2):
                    gp = slice(g * 64, (g + 1) * 64)
                    oc = (i * 2 + g) * 16
                    P.mm(stw[:, oc:oc + 16], KTw[:, i * 128:(i + 1) * 128], qzs[g][:, b, :], start=True, stop=(i != 0))
                    if i == 0:
                        P.mm(stw[:, oc:oc + 16], identb.all(), wb0, start=False, stop=True)
            PTw = tv(3104, [128, 128], BF16)
            P.act(PTw, stw[:, 0:128], AF.Exp, scale=0.125)
            for i in range(4):
                for g in range(2):
                    oc = (i * 2 + g) * 16
                    P.mm(owin[g][0:16, 0:65], PTw[:, oc:oc + 16], V1g[:, i, g, 0:65], start=(i == 0), stop=False)
            for bi, (kch, og) in enumerate(((26, osel), (27, owin))):
                stn = PS[0]
                for g in range(2):
                    gp = slice(g * 64, (g + 1) * 64)
                    P.mm(stn[0:NS, g * 16:(g + 1) * 16], HT.p(kch)[:, kch, 0:NS], qzs[g][:, b, :], start=True, stop=False)
                    P.mm(stn[0:NS, g * 16:(g + 1) * 16], identb[:, 0:NS], nbias[:, b, :], start=False, stop=True)
                for g in range(2):
                    P.act(PTn[0:NS], stn[0:NS, g * 16:(g + 1) * 16], AF.Exp, scale=0.125)
                    P.mm(og[g][0:16, 0:65], PTn, Vnew1[:, bi, g, 0:65], start=False, stop=True)
                    P.copy(Osb[:, 1 + bi, g, :], og[g][0:16, 0:65], eng="scalar")
            G16 = tv(4444, [16, 24]); gm = tv(4468, [16, 24]); gsel = tv(4492, [16, 6]); dn = tv(4498, [16, 2])
            y16 = tv(4504, [16, 2, 64]); tmpy = tv(4632, [16, 2, 64]); Y2 = tv(4760, [16, 128])
            gpb = PS[1]
            P.mm(gpb[0:16, 0:24], Selb[:, b, :], gate[0:NS, 0, :])
            P.copy(G16, gpb[0:16, 0:24], eng="scalar")
            P.tt(gm.rearrange("p (g r b) -> p g r b", g=2, r=4), G16.rearrange("p (g r b) -> p g r b", g=2, r=4),
                 rm[:, 2:6].map(lambda a: a.unsqueeze(1).unsqueeze(3).broadcast_to([16, 2, 4, 3])), ALU.mult)
            P.op("vector", "tensor_reduce", out=gsel.rearrange("p (g b) -> p g b", g=2), in_=gm.rearrange("p (g r b) -> p g b r", g=2, r=4), axis=AX.X, op=ALU.add)
            for bi in range(3):
                P.ts(dn, Osb[:, bi, :, 64], 1e-30, ALU.max)
                P.op("vector", "reciprocal", out=dn, in_=dn)
                P.tt(dn, dn, gsel.rearrange("p (g b) -> p g b", g=2)[:, :, bi], ALU.mult)
                dst = y16 if bi == 0 else tmpy
                P.tt(dst, Osb[:, bi, :, 0:64], dn.map(lambda a: a.unsqueeze(2).broadcast_to([16, 2, 64])), ALU.mult)
                if bi > 0:
                    P.tt(y16, y16, tmpy, ALU.add)
            for g in range(2):
                P.ts(Y2[:, 0:64], y16[:, g, :], rm[:, 0:1], ALU.mult)
                P.ts(Y2[:, 64:128], y16[:, g, :], rm[:, 1:2], ALU.mult)
                tpy_ps = PS[0]
                P.op("tensor", "transpose", out=tpy_ps[:, 0:16], in_=Y2, identity=identf[0:16, 0:16])
                tpy = tv(6080, [128, 16])
                P.copy(tpy, tpy_ps[:, 0:16], eng="scalar")
                for rr_ in range(2):
                    ch = 20 + 2 * g + rr_
                    P.tt(HT.p(ch)[:, ch, 4 * b:4 * b + 4], tpy[:, (2 * rr_) * 4:(2 * rr_) * 4 + 4], tpy[:, (2 * rr_ + 1) * 4:(2 * rr_ + 1) * 4 + 4], ALU.add)

    def sample_layer(l):
        src = x_s if l == 0 else xs1d
        P.dma(ACC.p(0)[0:NS, 0, :], src.all())
        transpose_to_T(XT, 1, NS)
        kvtok = tv(2900, [128, NT, 792])
        def fm_s(col0, ncols, chunk_ids, dsts, tok=False, tok_off=0):
            banks = [bank() for _ in chunk_ids]
            tb = bank() if tok else None
            for half in range(2):
                wv = wload(w_in, l, half * 1024, 8, col0, ncols)
                for ci, j in enumerate(chunk_ids):
                    for k in range(8):
                        kk = half * 8 + k
                        P.mm(banks[ci][:, 0:NS], wv[:, k, j * 128:(j + 1) * 128], XT.p(kk)[:, kk, 0:NS], start=(kk == 0), stop=(kk == 15))
                if tok:
                    for k in range(8):
                        kk = half * 8 + k
                        P.mm(tb[0:NS, 0:ncols], XT.p(kk)[:, kk, 0:NS], wv[:, k, :], start=(kk == 0), stop=(kk == 15))
            for ci, dv in enumerate(dsts):
                P.copy(dv, banks[ci][:, 0:NS], eng=evac_eng())
            if tok:
                P.copy(kvtok[0:NS, 0, tok_off:tok_off + ncols], tb[0:NS, 0:ncols], eng=evac_eng())
        for blk in range(6):
            fm_s(blk * 512, 512, [0, 1, 2, 3], [HT.p(blk * 4 + j)[:, blk * 4 + j, 0:NS] for j in range(4)])
        fm_s(3072, 512, [0, 1, 2], [HT.p(24 + j)[:, 24 + j, 0:NS] for j in range(3)], tok=True, tok_off=0)
        fm_s(3584, 280, [0], [HT.p(27)[:, 27, 0:NS]], tok=True, tok_off=512)
        P.dma(kvs[l], kvtok[0:NS, 0, 0:512], is_output=True)
        P.dma(wins[l][:, 0:508, :], cwin[l][:, 4:512, :], is_output=True)
        for b in range(NB):
            P.dma(wins[l, b, 508:512, :], kvtok[4 * b:4 * b + 4, 0, 512:768], is_output=True)
        P.act(gate[0:NS, 0, :], kvtok[0:NS, 0, 768:792], AF.Exp, scale=-1.0)
        P.ts(gate[0:NS, 0, :], gate[0:NS, 0, :], 1.0, ALU.add)
        P.op("vector", "reciprocal", out=gate[0:NS, 0, :], in_=gate[0:NS, 0, :])
        for b in range(NB):
            for two in range(2):
                pr = slice(two * 64, (two + 1) * 64)
                P.dma(sst[pr, :, b, 0], st_re[l, b].rearrange("(m two) n -> two n m", two=2)[two], allow_slow_non_contiguous=True)
                P.dma(sst[pr, :, b, 1], st_im[l, b].rearrange("(m two) n -> two n m", two=2)[two], allow_slow_non_contiguous=True)
            for kcc in range(4):
                P.dma(convc_s[:, kcc, b, :], st_conv[l, b][:, kcc * 128:(kcc + 1) * 128].rearrange("j p -> p j"), allow_slow_non_contiguous=True)
                P.dma(poolc_s[:, kcc, b, :], st_pool[l, b][:, kcc * 128:(kcc + 1) * 128].rearrange("j p -> p j"), allow_slow_non_contiguous=True)
        if "s5" in parts:
            s5_mixer(l, NB, 4, False, sst.all(), NS)
        if "conv" in parts:
            conv_mixer(NB, 4, convc_s.all(), NS)
        if "pool" in parts:
            pool_mixer(NB, 4, poolc_s.all(), NS, first_chunk=False)
        if "nsa" in parts:
            nsa_sample(l, kvtok)
        for b in range(NB):
            for two in range(2):
                pr = slice(two * 64, (two + 1) * 64)
                P.dma(sre_s[l, b].rearrange("(m two) n -> two n m", two=2)[two], sst[pr, :, b, 0], is_output=True, allow_slow_non_contiguous=True)
                P.dma(sim_s[l, b].rearrange("(m two) n -> two n m", two=2)[two], sst[pr, :, b, 1], is_output=True, allow_slow_non_contiguous=True)
            for kcc in range(4):
                P.dma(conv_s[l, b][:, kcc * 128:(kcc + 1) * 128].rearrange("j p -> p j"), convc_s[:, kcc, b, :], is_output=True, allow_slow_non_contiguous=True)
                P.dma(pool_s[l, b][:, kcc * 128:(kcc + 1) * 128].rearrange("j p -> p j"), poolc_s[:, kcc, b, :], is_output=True, allow_slow_non_contiguous=True)
        if DEBUG:
            P.dma(d["dbgs"][:, 0:24], HT.all()[:, 0:24, 0:NS], eng="gpsimd", is_output=True)
        if "ffn" not in parts:
            return
        dense_tail(l, 1, NS, NS)
        dst = xs1d if l < n_layers - 1 else y_s
        P.dma(dst.all(), ACC.p(0)[0:NS, 0, :], is_output=(l == n_layers - 1))

    for l in range(n_layers):
        layer_setup(l)
        for c in range(n_chunks):
            prompt_chunk(l, c)
        if do_sample:
            sample_layer(l)
    P.emit()
    P.close()
    return nc, P

_CACHE = {}

def _consts():
    k = np.arange(128)
    caus = np.where(k[:, None] > k[None, :], NEG, 0.0).astype(np.float32)
    anti = np.where(k[:, None] <= k[None, :], NEG, 0.0).astype(np.float32)
    ebig = (np.arange(128)[:, None] == (np.arange(8192)[None, :] // 64)).astype(np.float32)
    j = np.arange(64)
    tmask = np.zeros((32, 128, 3, 64), np.float32)
    for T in range(32):
        qpos = T * 128 + np.arange(128)
        complete = ((j[None, :] + 1) * 64 - 1) <= qpos[:, None]
        cur = qpos // 64
        forced = (j[None, :] == 0) | (j[None, :] == cur[:, None]) | (j[None, :] == cur[:, None] - 1)
        started = (j[None, :] * 64) <= qpos[:, None]
        tmask[T, :, 0, :] = np.where(complete, 0.0, NEG)
        tmask[T, :, 1, :] = (started & ~forced).astype(np.float32)
        tmask[T, :, 2, :] = np.where(forced, 5.0, np.where(started, 0.0, -1.0))
    cmpbT = np.ascontiguousarray(tmask[:, :, 0, :].transpose(0, 2, 1))
    rc = np.zeros((128, 4, 16), np.float32)
    for kc, w in enumerate((2, 4, 8, 16)):
        rc[:, kc, :] = 1.0 / np.minimum(np.arange(16) + 1, w)
    NB = 32 // NCORES; NS = NB * 4
    ibig = (np.arange(254)[None, :] == (126 + np.arange(128)[:, None] // 64)).astype(np.float32)
    nb = np.full((NS, NB, 16), NEG, np.float32); selb = np.zeros((NS, NB, 16), np.float32)
    for b in range(NB):
        for r in range(4):
            for t in range(4):
                selb[4 * b + t, b, r * 4 + t] = 1.0
                for kt in range(t + 1):
                    nb[4 * b + kt, b, r * 4 + t] = 0.0
    wb0 = np.zeros((128, 16), np.float32)
    for r in range(4):
        for t in range(4):
            wb0[:t + 1, r * 4 + t] = NEG
    rmk = np.zeros((16, 6), np.float32)
    for r in range(4):
        for t in range(4):
            rmk[r * 4 + t, r % 2] = 1.0
            rmk[r * 4 + t, 2 + r] = 1.0
    extra = {"c_iota": np.arange(128, dtype=np.float32).reshape(128, 1), "c_ibig": ibig, "c_nb": nb, "c_wb0": wb0, "c_selb": selb, "c_rm": rmk}
    return {**extra, "c_ident": np.eye(128, dtype=np.float32), "c_caus": caus, "c_anti": anti, "c_ebig": ebig,
            "c_tmask": tmask, "c_cmpbT": cmpbT, "c_rcnt0": rc}


def _perm_w_in(w):
    w = np.array(w, dtype=np.float32, copy=True)
    q = w[:, :, 2560:3072].reshape(2, D, 8, 64)
    w[:, :, 2560:3072] = q[:, :, [0, 4, 1, 5, 2, 6, 3, 7]].reshape(2, D, 512)
    return w


def make_in_maps(inp):
    C = _consts()
    f = lambda a: np.ascontiguousarray(a)
    shared = {
        "cache": f(inp["cache_nsa_kv"]).reshape(2, 2560 * 128, 512)[:, 0:CACHE_ROWS],
        "w_in": _perm_w_in(inp["w_in"]), "w_out": f(inp["w_out"]), "w_up": f(inp["w_up"]), "w_down": f(inp["w_down"]),
        "a_re": f(inp["ssm_a_re"]), "a_im": f(inp["ssm_a_im"]), "ldt": f(inp["ssm_log_dt"]),
        "b_re": f(inp["ssm_b_re"]), "b_im": f(inp["ssm_b_im"]), "c_re": f(inp["ssm_c_re"]), "c_im": f(inp["ssm_c_im"]),
        "ssm_d": f(inp["ssm_d"]), "w_glu": f(inp["ssm_w_glu"]), "b_glu": f(inp["ssm_b_glu"]),
        "conv_w": f(inp["conv_w"]), "conv_b": f(inp["conv_b"]), "pool_w": f(inp["pool_w"]), "pool_sc": f(inp["pool_scale"]),
        "wck": f(inp["nsa_w_cmp_k"]), "wcv": f(inp["nsa_w_cmp_v"]),
        "ln1g": f(inp["ln1_g"]), "ln1b": f(inp["ln1_b"]), "ln2g": f(inp["ln2_g"]), "ln2b": f(inp["ln2_b"]),
    }
    shared.update(C)
    maps = []
    NB = 32 // NCORES
    for c in range(NCORES):
        b = c % 2
        sb = slice(NB * c, NB * c + NB)
        m = dict(shared)
        m["x_p"] = f(inp["x_prompt"][b])
        m["x_s"] = f(inp["x_sample"][sb]).reshape(NB * 4, D)
        m["cwin"] = f(inp["cache_win_kv"][:, sb]).reshape(2, NB, 512, 256)
        m["st_re"] = f(inp["state_ssm_re"][:, sb]); m["st_im"] = f(inp["state_ssm_im"][:, sb])
        m["st_conv"] = f(inp["state_conv"][:, sb]); m["st_pool"] = f(inp["state_pool"][:, sb])
        m["ptab"] = f(inp["page_table"][sb]).astype(np.int32)
        maps.append(m)
    return maps


def assemble(res):
    r = res
    y_p = np.stack([r[0]["y_p"], r[1]["y_p"]])
    cat = lambda key, ax: np.concatenate([r[c][key] for c in range(NCORES)], axis=ax)
    y_s = cat("y_s", 0).reshape(32, 4, D)
    kvp = np.stack([r[0]["kvp"], r[1]["kvp"]], axis=1).reshape(2, 2, SEQ, 4, 2, 64)
    kvs = cat("kvs", 1).reshape(2, 32, 4, 4, 2, 64)
    winp = np.stack([r[0]["winp"], r[1]["winp"]], axis=1).reshape(2, 2, 512, 2, 2, 64)
    wins = cat("wins", 1).reshape(2, 32, 512, 2, 2, 64)
    st2 = lambda key: np.stack([r[0][key], r[1][key]], axis=1)
    return (y_p, y_s, kvp, kvs, winp, wins, st2("sre_p"), st2("sim_p"), cat("sre_s", 1), cat("sim_s", 1),
            st2("conv_p"), cat("conv_s", 1), st2("pool_p"), cat("pool_s", 1))


def kernel(**inputs):
    if "nc" not in _CACHE:
        _CACHE["nc"] = build_program()[0]
    maps = make_in_maps(inputs)
    res = run_bass_kernel_spmd(_CACHE["nc"], maps, core_ids=list(range(NCORES)))
    return assemble(res.results)
```

```python
import contextlib
import numpy as np
import concourse.bass as bass
import concourse.mybir as mybir

F32 = mybir.dt.float32
BF16 = mybir.dt.bfloat16
I32 = mybir.dt.int32
U32 = mybir.dt.uint32
ALU = mybir.AluOpType
AF = mybir.ActivationFunctionType
AX = mybir.AxisListType

COMPUTE = ("tensor", "vector", "scalar", "gpsimd")
NDMA_SEM = {"sync": 12, "gpsimd": 8}


class Tile:
    def __init__(self, name, ap, nparts=1):
        self.name = name
        self.ap = ap
        self.nparts = nparts

    def __getitem__(self, idx):
        return View(self.ap[idx], [(self, p) for p in range(self.nparts)])

    def p(self, *parts):
        return _PartProxy(self, parts)

    def all(self):
        return View(self.ap, [(self, p) for p in range(self.nparts)])


class _PartProxy:
    def __init__(self, tile, parts):
        self.tile, self.parts = tile, parts

    def __getitem__(self, idx):
        return View(self.tile.ap[idx], [(self.tile, p) for p in self.parts])


class View:
    def __init__(self, ap, res):
        self.ap = ap
        self.res = res

    def __getitem__(self, idx):
        return View(self.ap[idx], self.res)

    def rearrange(self, *a, **k):
        return View(self.ap.rearrange(*a, **k), self.res)

    def bitcast(self, dt):
        return View(self.ap.bitcast(dt), self.res)

    def map(self, f):
        return View(f(self.ap), self.res)


class _Op:
    __slots__ = ("eng", "fn", "deps", "signal", "count", "dma_sem", "dma_val", "dma_prev", "name")


class Prog:
    def __init__(self, nc, same_engine_sync=True):
        self.nc = nc
        self.ops = {e: [] for e in ("tensor", "vector", "scalar", "gpsimd", "sync")}
        self.last_w = {}
        self.readers = {}
        self.same_engine_sync = same_engine_sync
        self.dma_count = {"sync": 0, "gpsimd": 0}
        self.dma_sem_total = {}
        self.stack = contextlib.ExitStack()
        self.out_dma = []

    def sbuf(self, name, shape, dtype, nparts=1):
        t = self.stack.enter_context(self.nc.sbuf_tensor(name, list(shape), dtype))
        return Tile(name, t[:], nparts)

    def psum(self, name, shape, dtype, nparts=1):
        t = self.stack.enter_context(self.nc.psum_tensor(name, list(shape), dtype))
        return Tile(name, t[:], nparts)

    def dram(self, name, shape, dtype, kind, nparts=1, **kw):
        t = self.nc.dram_tensor(name, list(shape), dtype, kind=kind, **kw)
        return Tile(name, t.ap(), nparts)

    def _record(self, eng, fn, reads, writes, is_dma=False, name=None):
        op = _Op()
        op.eng, op.fn, op.signal, op.count, op.name = eng, fn, False, None, name
        op.dma_sem = op.dma_val = op.dma_prev = None
        idx = len(self.ops[eng])
        me = (eng, idx)
        deps = set()
        for r in reads:
            w = self.last_w.get(r)
            if w is not None:
                deps.add(w)
        for w_ in writes:
            w = self.last_w.get(w_)
            if w is not None:
                deps.add(w)
            for rd in self.readers.get(w_, ()):
                deps.add(rd)
        deps.discard(me)
        for r in reads:
            self.readers.setdefault(r, []).append(me)
        for w_ in writes:
            self.last_w[w_] = me
            self.readers[w_] = []
        if is_dma:
            k = self.dma_count[eng]
            self.dma_count[eng] += 1
            slot = (eng, k % NDMA_SEM[eng])
            prev = self.dma_sem_total.get(slot, 0)
            op.dma_sem = slot
            op.dma_prev = prev
            op.dma_val = prev + 16
            self.dma_sem_total[slot] = prev + 16
        op.deps = deps
        self.ops[eng].append(op)
        return op

    @staticmethod
    def _split(kwargs, out_names):
        reads, writes, call = [], [], {}
        for k, v in kwargs.items():
            if isinstance(v, View):
                (writes if k in out_names else reads).extend(v.res)
                call[k] = v.ap
            else:
                call[k] = v
        return reads, writes, call

    def op(self, eng, method, _extra_reads=(), _extra_writes=(), **kwargs):
        reads, writes, call = self._split(kwargs, ("out", "accum_out"))
        for v in _extra_reads:
            reads.extend(v.res)
        for v in _extra_writes:
            writes.extend(v.res)

        def fn(e, method=method, call=call):
            return getattr(e, method)(**call)

        return self._record(eng, fn, reads, writes, name=method)

    def custom(self, eng, fn, reads=(), writes=(), name=None):
        r, w = [], []
        for v in reads:
            r.extend(v.res)
        for v in writes:
            w.extend(v.res)
        return self._record(eng, fn, r, w, name=name)

    def dma(self, out, in_, eng="sync", is_output=False, **kw):
        def fn(e, o=out.ap, i=in_.ap, kw=kw):
            return e.dma_start(out=o, in_=i, **kw)

        op = self._record(eng, fn, list(in_.res), list(out.res), is_dma=True, name="dma")
        if is_output:
            self.out_dma.append(op)
        return op

    def dma_custom(self, fn, reads, writes, eng="gpsimd"):
        r, w = [], []
        for v in reads:
            r.extend(v.res)
        for v in writes:
            w.extend(v.res)
        return self._record(eng, fn, r, w, is_dma=True, name="dmac")

    def mm(self, out, lhsT, rhs, start=True, stop=True, **kw):
        return self.op("tensor", "matmul", out=out, lhsT=lhsT, rhs=rhs, start=start, stop=stop, **kw)

    def act(self, out, in_, func, eng="scalar", **kw):
        return self.op(eng, "activation", out=out, in_=in_, func=func, **kw)

    def tt(self, out, in0, in1, op, eng="vector"):
        return self.op(eng, "tensor_tensor", out=out, in0=in0, in1=in1, op=op)

    def ts(self, out, in0, s1, op0, s2=None, op1=None, eng="vector", **kw):
        if op1 is None:
            return self.op(eng, "tensor_scalar", out=out, in0=in0, scalar1=s1, scalar2=None, op0=op0, **kw)
        return self.op(eng, "tensor_scalar", out=out, in0=in0, scalar1=s1, scalar2=s2, op0=op0, op1=op1, **kw)

    def stt(self, out, in0, scalar, in1, op0, op1, eng="vector"):
        return self.op(eng, "scalar_tensor_tensor", out=out, in0=in0, scalar=scalar, in1=in1, op0=op0, op1=op1)

    def copy(self, out, in_, eng="vector"):
        if eng == "scalar":
            return self.op("scalar", "activation", out=out, in_=in_, func=AF.Copy)
        return self.op(eng, "tensor_copy", out=out, in_=in_)

    def memset(self, out, val, eng="vector"):
        return self.op(eng, "memset", _extra_writes=[out], ap=out.ap, constant=val)

    def emit(self):
        nc = self.nc
        for eng, lst in self.ops.items():
            for op in lst:
                for (de, di) in op.deps:
                    d = self.ops[de][di]
                    if d.dma_sem is None:
                        d.signal = True
        for eng, lst in self.ops.items():
            c = 0
            for op in lst:
                if op.dma_sem is None and op.signal:
                    c += 1
                    op.count = c
        sems = {}
        st = self.stack
        for e in COMPUTE + ("sync",):
            sems[e] = st.enter_context(nc.semaphore("s_" + e))
        dsems = {}
        for q, n in NDMA_SEM.items():
            for i in range(n):
                dsems[(q, i)] = st.enter_context(nc.semaphore("d_%s%d" % (q, i)))
        ops_all = self.ops
        same = self.same_engine_sync
        final = dict(self.dma_sem_total)

        def run_engine(ename, e):
            waited = {}
            for idx, op in enumerate(ops_all[ename]):
                need = {}
                for (de, di) in op.deps:
                    d = ops_all[de][di]
                    if d.dma_sem is not None:
                        key = ("D",) + d.dma_sem
                        val = d.dma_val
                    else:
                        if de == ename:
                            if ename in ("tensor", "sync"):
                                continue
                            if not same:
                                continue
                        key = ("E", de)
                        val = d.count
                    if need.get(key, 0) < val:
                        need[key] = val
                if op.dma_sem is not None and op.dma_prev > 0:
                    key = ("D",) + op.dma_sem
                    if need.get(key, 0) < op.dma_prev:
                        need[key] = op.dma_prev
                for key, val in need.items():
                    if waited.get(key, 0) >= val:
                        continue
                    waited[key] = val
                    s = dsems[key[1:]] if key[0] == "D" else sems[key[1]]
                    e.wait_ge(s, val)
                inst = op.fn(e)
                if op.dma_sem is not None:
                    inst.then_inc(dsems[op.dma_sem], 16)
                elif op.signal:
                    inst.then_inc(sems[ename], 1)
            if ename == "sync":
                for slot, val in final.items():
                    e.wait_ge(dsems[slot], val)

        with nc.allow_low_precision(reason='bf16 matmul operands by design'), nc.Block() as block:
            @block.sync
            def _(e):
                run_engine("sync", e)

            @block.gpsimd
            def _(e):
                run_engine("gpsimd", e)

            @block.tensor
            def _(e):
                run_engine("tensor", e)

            @block.vector
            def _(e):
                run_engine("vector", e)

            @block.scalar
            def _(e):
                run_engine("scalar", e)

    def close(self):
        self.stack.close()

    def stats(self):
        return {e: len(l) for e, l in self.ops.items()}

import math
from concourse.bass_utils import run_bass_kernel_spmd

D = 2048
NCOL = 3864
DFF = 8192
NCH = 512
NT = 4
SEQ = 4096
NCHUNK = SEQ // NCH
ALPHA_ = (2.0 * 2) ** 0.25
NEG = -30000.0
LSEG = 256
GC = 2.0 * math.sqrt(2.0 / math.pi)
CACHE_ROWS = 2560 * 128
NCORES = 2
DEBUG = False
NSA_LEVEL = 9


def build_program(n_layers=2, n_chunks=NCHUNK, do_sample=True, parts=("s5", "conv", "pool", "nsa", "ffn")):
    NB = 32 // NCORES
    NS = NB * 4
    nc = bass.Bass("TRN2", target_bir_lowering=False)
    P = Prog(nc)
    EI, EO, IN = "ExternalInput", "ExternalOutput", "Internal"
    d = {}
    def din(name, shape, dt=F32):
        d[name] = P.dram(name, shape, dt, EI)
        return d[name]
    def dout(name, shape, dt=F32):
        d[name] = P.dram(name, shape, dt, EO)
        return d[name]
    x_p = din("x_p", [SEQ, D]); x_s = din("x_s", [NS, D])
    cache = din("cache", [2, CACHE_ROWS, 512]); cwin = din("cwin", [2, NB, 512, 256])
    st_re = din("st_re", [2, NB, 32, 64]); st_im = din("st_im", [2, NB, 32, 64])
    st_conv = din("st_conv", [2, NB, 2, 512]); st_pool = din("st_pool", [2, NB, 15, 512])
    ptab = din("ptab", [NB, 64], I32)
    w_in = din("w_in", [2, D, NCOL]); w_out = din("w_out", [2, D, D]); w_up = din("w_up", [2, D, DFF]); w_down = din("w_down", [2, DFF, D])
    a_re = din("a_re", [2, 32, 64]); a_im = din("a_im", [2, 32, 64]); ldt = din("ldt", [2, 32])
    b_re = din("b_re", [2, 32, 64, 16]); b_im = din("b_im", [2, 32, 64, 16])
    c_re = din("c_re", [2, 32, 16, 64]); c_im = din("c_im", [2, 32, 16, 64])
    ssm_d = din("ssm_d", [2, 512]); w_glu = din("w_glu", [2, 512, 512]); b_glu = din("b_glu", [2, 512])
    conv_w = din("conv_w", [2, 3, 512]); conv_b = din("conv_b", [2, 512])
    pool_w = din("pool_w", [2, 4, 128, 128]); pool_sc = din("pool_sc", [2, 512])
    wck = din("wck", [2, 64, 64]); wcv = din("wcv", [2, 64, 64])
    ln1g = din("ln1g", [2, D]); ln1b = din("ln1b", [2, D]); ln2g = din("ln2g", [2, D]); ln2b = din("ln2b", [2, D])
    c_ident = din("c_ident", [128, 128])
    c_caus = din("c_caus", [128, 128]); c_anti = din("c_anti", [128, 128])
    c_ebig = din("c_ebig", [128, 8192])
    c_tmask = din("c_tmask", [32, 128, 3, 64])
    c_cmpbT = din("c_cmpbT", [32, 64, 128])
    c_rcnt0 = din("c_rcnt0", [128, 4, 16])
    din("c_iota", [128, 1]); din("c_ibig", [128, 254]); din("c_nb", [NS, NB, 16]); din("c_wb0", [128, 16]); din("c_selb", [NS, NB, 16]); din("c_rm", [16, 6])
    y_p = dout("y_p", [SEQ, D]); y_s = dout("y_s", [NS, D])
    kvp = dout("kvp", [2, SEQ, 512]); kvs = dout("kvs", [2, NS, 512])
    winp = dout("winp", [2, 512, 256]); wins = dout("wins", [2, NB, 512, 256])
    sre_p = dout("sre_p", [2, 32, 64]); sim_p = dout("sim_p", [2, 32, 64])
    sre_s = dout("sre_s", [2, NB, 32, 64]); sim_s = dout("sim_s", [2, NB, 32, 64])
    conv_p = dout("conv_p", [2, 2, 512]); conv_s = dout("conv_s", [2, NB, 2, 512])
    pool_p = dout("pool_p", [2, 15, 512]); pool_s = dout("pool_s", [2, NB, 15, 512])
    if DEBUG:
        dout("dbg", [NCHUNK, 128, 28, NCH]); dout("dbgs", [128, 28, NS])
    x1d = P.dram("x1d", [SEQ, D], F32, IN)
    tabd = P.dram("tabd", [2, 128, 16, 2, LSEG], F32, IN)
    ctd = P.dram("ctd", [2, 128, 16, 2, 128], F32, IN)

    WS = [P.sbuf("ws%d" % i, [128, 4096], BF16) for i in range(4)]
    XT = P.sbuf("xT", [128, 16, NCH], BF16, nparts=16)
    HT = P.sbuf("hT", [128, 28, NCH], BF16, nparts=28)
    ACC = P.sbuf("acc", [128, NT, D], F32, nparts=NT)
    TMP = P.sbuf("tmp", [128, 6144], F32, nparts=24)
    identf = P.sbuf("identf", [128, 128], F32); identb = P.sbuf("identb", [128, 128], BF16)
    caus4 = P.sbuf("caus4", [128, 4, 128], BF16); anti4 = P.sbuf("anti4", [128, 4, 128], BF16)
    ebig = P.sbuf("ebig", [128, 4096], BF16)
    BT = P.sbuf("BT", [128, 16, 2, 128], BF16)
    wglu = P.sbuf("wglu", [128, 4, 512], BF16); poolw = P.sbuf("poolw", [128, 4, 128], BF16)
    sc = P.sbuf("sc", [128, 64], F32)
    mag = P.sbuf("mag", [128, 16], F32)
    wckT = P.sbuf("wckT", [128, 64], F32); wcvT = P.sbuf("wcvT", [128, 64], F32)
    KselT = P.sbuf("KselT", [128, SEQ], BF16, nparts=32)
    Vsel1 = P.sbuf("Vsel1", [128, 32, 2, 72], BF16, nparts=32)
    KwinT = P.sbuf("KwinT", [128, 8, 128], BF16, nparts=8)
    Vwin1 = P.sbuf("Vwin1", [128, 8, 2, 72], BF16, nparts=8)
    kcT = P.sbuf("kcT", [128, 64], BF16); vcT = P.sbuf("vcT", [128, 64], BF16)
    vc1 = P.sbuf("vc1", [128, 2, 72], BF16)
    kcTz = P.sbuf("kcTz", [128, 2, 128], BF16)
    hst = P.sbuf("hst", [128, 16, 2], F32)
    convc = P.sbuf("convc", [128, 4, 2], F32); poolc = P.sbuf("poolc", [128, 4, 15], F32)
    gate = P.sbuf("gate", [128, NT, 24], F32)
    PS = [P.psum("ps%d" % i, [128, 512], F32) for i in range(8)]

    tmp = TMP
    def tv(off, shape, dt=F32):
        n = 1
        for s in shape[1:]:
            n *= s
        nw = (n + 1) // 2 if dt == BF16 else n
        assert off + nw <= 6144, (off, shape)
        pp = tmp.p(*range(off // 256, (off + nw - 1) // 256 + 1))
        if dt == BF16:
            v = pp[:, off:off + nw].bitcast(BF16)
            v = v[:, 0:n]
        else:
            v = pp[:, off:off + n]
        if shape[0] != 128:
            v = v[0:shape[0]]
        if len(shape) == 2:
            return v
        names = " ".join("a%d" % i for i in range(len(shape) - 1))
        kw = {"a%d" % i: shape[i + 1] for i in range(len(shape) - 1)}
        return v.rearrange("p (%s) -> p %s" % (names, names), **kw)

    bank_rr = [0]
    def bank():
        b = PS[bank_rr[0] % 8]
        bank_rr[0] += 1
        return b

    evac_rr = [0]
    def evac_eng():
        evac_rr[0] += 1
        return "scalar" if evac_rr[0] % 2 else "vector"

    wrr = [0]
    wscr = {}
    wseen = set()
    def wload(W, l, row0, nk, col0, ncols):
        slot = WS[wrr[0] % 4]
        wrr[0] += 1
        view = slot[:, 0:nk * ncols].rearrange("p (k n) -> p k n", n=ncols)
        if (W.name, l) not in wscr:
            shp = W.ap.shape
            wscr[(W.name, l)] = P.dram("wb_%s_%d" % (W.name, l), [shp[1], shp[2]], BF16, IN)
        scr = wscr[(W.name, l)][row0:row0 + nk * 128, col0:col0 + ncols].rearrange("(k p) n -> p k n", p=128)
        key = (W.name, l, row0, nk, col0, ncols)
        if key not in wseen:
            wseen.add(key)
            src = W[l, row0:row0 + nk * 128, col0:col0 + ncols].rearrange("(k p) n -> p k n", p=128)
            P.dma(view, src, eng="gpsimd")
            P.dma(scr, view)
        else:
            P.dma(view, scr)
        return view

    P.dma(identf.all(), c_ident.all())
    P.copy(identb.all(), identf.all())
    ct = tv(0, [128, 128])
    P.dma(ct, c_caus.all())
    P.copy(caus4.all(), ct.map(lambda a: a.unsqueeze(1).broadcast_to([128, 4, 128])))
    ct2 = tv(128, [128, 128])
    P.dma(ct2, c_anti.all())
    P.copy(anti4.all(), ct2.map(lambda a: a.unsqueeze(1).broadcast_to([128, 4, 128])))
    P.dma(ebig.all(), c_ebig[:, 0:4096], eng="gpsimd")

    def transpose_to_T(dst, ntok_tiles, rows=128):
        for i in range(ntok_tiles):
            for k0 in range(0, 16, 4):
                pb = bank()
                for k in range(4):
                    P.op("tensor", "transpose", out=pb[:, k * 128:k * 128 + rows],
                         in_=ACC.p(i)[0:rows, i, (k0 + k) * 128:(k0 + k + 1) * 128], identity=identf[0:rows, 0:rows])
                P.copy(dst.p(*range(k0, k0 + 4))[:, k0:k0 + 4, i * 128:i * 128 + rows],
                       pb.all().rearrange("p (k t) -> p k t", k=4)[:, :, 0:rows], eng=evac_eng())

    def layer_norm(i, gb, ntok=128):
        st = tv(6000, [128, 4, 6]); mv = tv(6030, [128, 2]); rs = tv(6040, [128, 1])
        xr = ACC.p(i)[0:ntok, i, :]
        for q in range(4):
            P.op("vector", "bn_stats", out=st[0:ntok, q, :], in_=xr[:, q * 512:(q + 1) * 512])
        P.op("vector", "bn_aggr", out=mv[0:ntok, :], in_=st[0:ntok].rearrange("p a b -> p (a b)"))
        P.ts(rs[0:ntok], mv[0:ntok, 1:2], 1e-5, ALU.add)
        P.act(rs[0:ntok], rs[0:ntok], AF.Ln)
        P.act(rs[0:ntok], rs[0:ntok], AF.Exp, scale=-0.5)
        P.ts(xr, xr, mv[0:ntok, 0:1], ALU.subtract, rs[0:ntok, 0:1], ALU.mult)
        P.tt(xr, xr, gb[0:ntok, 0, :], ALU.mult)
        P.tt(xr, xr, gb[0:ntok, 1, :], ALU.add)

    def load_gb(l, g_d, b_d):
        gb = tv(0, [128, 2, D])
        P.dma(gb[:, 0, :], g_d[l:l + 1, :].map(lambda a: a.broadcast_to([128, D])))
        P.dma(gb[:, 1, :], b_d[l:l + 1, :].map(lambda a: a.broadcast_to([128, D])))
        return gb

    def layer_setup(l):
        P.dma(sc[:, 0:4], ssm_d[l].rearrange("(k p) -> p k", p=128), allow_slow_non_contiguous=True)
        P.dma(sc[:, 4:8], b_glu[l].rearrange("(k p) -> p k", p=128), allow_slow_non_contiguous=True)
        P.ts(sc[:, 4:8], sc[:, 4:8], -1.0, ALU.mult)
        for j in range(3):
            P.dma(sc[:, 8 + 4 * j:12 + 4 * j], conv_w[l, j].rearrange("(k p) -> p k", p=128), allow_slow_non_contiguous=True)
        P.dma(sc[:, 20:24], conv_b[l].rearrange("(k p) -> p k", p=128), allow_slow_non_contiguous=True)
        P.dma(sc[:, 24:28], pool_sc[l].rearrange("(k p) -> p k", p=128), allow_slow_non_contiguous=True)
        P.dma(wglu.all(), w_glu[l].rearrange("(k p) n -> p k n", p=128), eng="gpsimd")
        P.dma(poolw.all(), pool_w[l].rearrange("g c d -> c g d"), eng="gpsimd")
        for two in range(2):
            P.dma(wckT[two * 64:(two + 1) * 64, :], wck[l].rearrange("k d -> d k"), allow_slow_non_contiguous=True)
            P.dma(wcvT[two * 64:(two + 1) * 64, :], wcv[l].rearrange("k d -> d k"), allow_slow_non_contiguous=True)
        P.memset(hst.all(), 0.0); P.memset(convc.all(), 0.0); P.memset(poolc.all(), 0.0)
        P.memset(kcT.all(), 0.0); P.memset(vcT.all(), 0.0)
        P.memset(vc1.all(), 1.0); P.memset(Vsel1.all(), 1.0); P.memset(Vwin1.all(), 1.0)
        s = lambda o, n=16: tv(o, [128, n])
        are, aim, dtt, th, cs, sn, t1, t2, abr, abi, den, fre, fim = [s(16 * i) for i in range(13)]
        for two in range(2):
            pr = slice(two * 64, (two + 1) * 64)
            P.dma(are[pr, :], a_re[l].rearrange("(m two) n -> two n m", two=2)[two], allow_slow_non_contiguous=True)
            P.dma(aim[pr, :], a_im[l].rearrange("(m two) n -> two n m", two=2)[two], allow_slow_non_contiguous=True)
            P.dma(dtt[pr, :], ldt[l:l + 1, :].rearrange("o (m two) -> o two m", two=2)[:, two, :].map(lambda a: a.broadcast_to([64, 16])), allow_slow_non_contiguous=True)
        P.act(dtt, dtt, AF.Exp)
        P.tt(t1, dtt, are, ALU.mult)
        P.act(mag.all(), t1, AF.Exp)
        P.tt(th, dtt, aim, ALU.mult)
        P.act(sn, th, AF.Sin, scale=1.0 / 16)
        P.ts(t2, th, 1.0 / 16, ALU.mult, math.pi / 2, ALU.add)
        P.act(cs, t2, AF.Sin)
        for _ in range(4):
            P.tt(t1, cs, cs, ALU.mult); P.tt(t2, sn, sn, ALU.mult)
            P.tt(sn, sn, cs, ALU.mult); P.ts(sn, sn, 2.0, ALU.mult)
            P.tt(cs, t1, t2, ALU.subtract)
        P.tt(abr, mag.all(), cs, ALU.mult); P.tt(abi, mag.all(), sn, ALU.mult)
        P.tt(t1, are, are, ALU.mult); P.tt(t2, aim, aim, ALU.mult); P.tt(den, t1, t2, ALU.add)
        P.op("vector", "reciprocal", out=den, in_=den)
        nr = t1
        P.ts(nr, abr, -1.0, ALU.add)
        P.tt(fre, nr, are, ALU.mult); P.tt(t2, abi, aim, ALU.mult); P.tt(fre, fre, t2, ALU.add); P.tt(fre, fre, den, ALU.mult)
        P.tt(fim, abi, are, ALU.mult); P.tt(t2, nr, aim, ALU.mult); P.tt(fim, fim, t2, ALU.subtract); P.tt(fim, fim, den, ALU.mult)
        bre = tv(256, [128, 16, 16]); bim = tv(512, [128, 16, 16]); bbr = tv(768, [128, 16, 16]); bbi = tv(1024, [128, 16, 16]); tb = tv(1280, [128, 16, 16])
        cre = tv(1536, [128, 16, 16]); cim = tv(1792, [128, 16, 16])
        for two in range(2):
            pr = slice(two * 64, (two + 1) * 64)
            P.dma(bre[pr], b_re[l].rearrange("(m two) n c -> two n m c", two=2)[two])
            P.dma(bim[pr], b_im[l].rearrange("(m two) n c -> two n m c", two=2)[two])
            for m in range(16):
                P.dma(cre[pr, m, :], c_re[l, 2 * m + two].rearrange("c n -> n c"), allow_slow_non_contiguous=True)
                P.dma(cim[pr, m, :], c_im[l, 2 * m + two].rearrange("c n -> n c"), allow_slow_non_contiguous=True)
        bc = lambda v: v.map(lambda a: a.unsqueeze(2).broadcast_to([128, 16, 16]))
        P.tt(bbr, bre, bc(fre), ALU.mult); P.tt(tb, bim, bc(fim), ALU.mult); P.tt(bbr, bbr, tb, ALU.subtract)
        P.tt(bbi, bim, bc(fre), ALU.mult); P.tt(tb, bre, bc(fim), ALU.mult); P.tt(bbi, bbi, tb, ALU.add)
        P.ts(cim, cim, -1.0, ALU.mult)
        Z = ACC.p(0, 1)[:, 0:2, :].rearrange("p a (m r c) -> p (a m) r c", r=2, c=128)
        Cz = ACC.p(2, 3)[:, 2:4, :].rearrange("p a (m r c) -> p (a m) r c", r=2, c=128)
        P.memset(ACC.p(0, 1)[:, 0:2, :], 0.0); P.memset(ACC.p(2, 3)[:, 2:4, :], 0.0)
        for ri, (srcB, srcC) in enumerate(((bbr, cre), (bbi, cim))):
            for q4 in range(4):
                for two in range(2):
                    pr = slice(two * 64, (two + 1) * 64)
                    co = q4 * 32 + two * 16
                    P.copy(Z[pr, q4::4, ri, co:co + 16], srcB[pr, q4::4, :])
                    P.copy(Cz[pr, q4::4, ri, co:co + 16], srcC[pr, q4::4, :], eng="scalar")
        P.dma(ctd[l], Cz)
        for m in range(16):
            pb = bank()
            for ri in range(2):
                P.op("tensor", "transpose", out=pb[:, ri * 128:(ri + 1) * 128], in_=Z[:, m, ri, :], identity=identf.all())
            P.copy(BT[:, m, :, :], pb[:, 0:256].rearrange("p (r c) -> p r c", r=2), eng=evac_eng())
        hview = HT.all().rearrange("p a n -> p (a n)").bitcast(F32)
        Er = hview[:, 0:16 * LSEG].rearrange("p (m t) -> p m t", t=LSEG)
        xv = XT.all().rearrange("p a n -> p (a n)").bitcast(F32)
        Ei = xv[:, 0:16 * LSEG].rearrange("p (m t) -> p m t", t=LSEG)
        ta = hview[:, 4096:6144].rearrange("p (m t) -> p m t", t=128); tb2 = tv(2200, [128, 16, 128])
        pr_, pi_ = tv(2048, [128, 16]), tv(2064, [128, 16])
        q1, q2 = tv(2080, [128, 16]), tv(2096, [128, 16])
        P.copy(Er[:, :, 0], cs); P.copy(Ei[:, :, 0], sn)
        P.copy(pr_, cs); P.copy(pi_, sn)
        n = 1
        while n < LSEG:
            bcn = lambda v: v.map(lambda a: a.unsqueeze(2).broadcast_to([128, 16, n]))
            P.tt(ta[:, :, 0:n], Er[:, :, 0:n], bcn(pr_), ALU.mult)
            P.tt(tb2[:, :, 0:n], Ei[:, :, 0:n], bcn(pi_), ALU.mult)
            P.tt(Er[:, :, n:2 * n], ta[:, :, 0:n], tb2[:, :, 0:n], ALU.subtract)
            P.tt(ta[:, :, 0:n], Er[:, :, 0:n], bcn(pi_), ALU.mult)
            P.tt(tb2[:, :, 0:n], Ei[:, :, 0:n], bcn(pr_), ALU.mult)
            P.tt(Ei[:, :, n:2 * n], ta[:, :, 0:n], tb2[:, :, 0:n], ALU.add)
            P.tt(q1, pr_, pr_, ALU.mult); P.tt(q2, pi_, pi_, ALU.mult)
            P.tt(pi_, pr_, pi_, ALU.mult); P.ts(pi_, pi_, 2.0, ALU.mult)
            P.tt(pr_, q1, q2, ALU.subtract)
            n *= 2
        P.dma(tabd[l][:, :, 0, :], Er); P.dma(tabd[l][:, :, 1, :], Ei)

    def s5_mixer(l, nseq, seqlen, chained, state_tile, ntok):
        ysb = tv(3072, [128, ntok])
        rr = [0]
        for kcc in range(4):
            ctl = tv(0, [128, 4, 2, 128])
            P.dma(ctl, ctd[l][:, kcc * 4:(kcc + 1) * 4])
            tab4 = tv(1024, [128, 4, 2, seqlen])
            for mm_ in range(4):
                P.dma(tab4[:, mm_], tabd[l][:, kcc * 4 + mm_, :, 0:seqlen])
            uT = HT.p(kcc)[:, kcc, 0:ntok]
            for sq in range(nseq):
                cols = slice(sq * seqlen, (sq + 1) * seqlen)
                ybank = PS[6 + sq % 2]
                for mm_ in range(4):
                    m = kcc * 4 + mm_
                    ErL = tab4[:, mm_, 0:1, :].map(lambda a: a.broadcast_to([128, 2, seqlen]))
                    EiL = tab4[:, mm_, 1:2, :].map(lambda a: a.broadcast_to([128, 2, seqlen]))
                    pa = PS[(2 * rr[0]) % 6]; pbk = PS[(2 * rr[0] + 1) % 6]
                    rr[0] += 1
                    L2 = 2 * seqlen
                    P.mm(pa[:, 0:seqlen], BT[:, m, 0, :], uT[:, cols]); P.mm(pa[:, seqlen:L2], BT[:, m, 1, :], uT[:, cols])
                    P.mm(pbk[:, 0:seqlen], BT[:, m, 1, :], uT[:, cols]); P.mm(pbk[:, seqlen:L2], BT[:, m, 0, :], uT[:, cols])
                    t1 = tv(3584, [128, 2, seqlen]); t2 = tv(4096, [128, 2, seqlen])
                    G3 = tv(4608, [128, 3, seqlen]); hh = tv(5376, [128, 2, seqlen])
                    P.tt(t1, pa[:, 0:L2].rearrange("p (r t) -> p r t", r=2), ErL, ALU.mult)
                    P.tt(t2, pbk[:, 0:L2].rearrange("p (r t) -> p r t", r=2), EiL, ALU.mult)
                    P.tt(t1[:, 0, :], t1[:, 0, :], t2[:, 0, :], ALU.add)
                    P.tt(t1[:, 1, :], t1[:, 1, :], t2[:, 1, :], ALU.subtract)
                    si = (sq if not chained else 0)
                    magb = mag[:, m:m + 1].map(lambda a: a.broadcast_to([128, seqlen]))
                    for r in range(2):
                        P.op("vector", "tensor_tensor_scan", out=G3[:, r, :], data0=magb, data1=t1[:, r, :],
                             initial=state_tile[:, m, si, r:r + 1], op0=ALU.mult, op1=ALU.add)
                    P.copy(G3[:, 2, :], G3[:, 0, :], eng="scalar")
                    P.tt(t1, G3[:, 0:2, :], ErL, ALU.mult)
                    P.tt(t2, G3[:, 1:3, :], EiL, ALU.mult)
                    P.tt(hh[:, 0, :], t1[:, 0, :], t2[:, 0, :], ALU.subtract)
                    P.tt(hh[:, 1, :], t1[:, 1, :], t2[:, 1, :], ALU.add)
                    P.copy(state_tile[:, m, si, :], hh[:, :, seqlen - 1], eng="scalar")
                    for r in range(2):
                        P.mm(ybank[:, 0:seqlen], ctl[:, mm_, r, :], hh[:, r, :], start=(mm_ == 0 and r == 0), stop=(mm_ == 3 and r == 1))
                P.stt(ysb[:, cols], uT[:, cols], sc[:, kcc:kcc + 1], ybank[:, 0:seqlen], ALU.mult, ALU.add)
            ys = ysb
            g1 = tv(3584, [128, ntok]); g2 = tv(4096, [128, ntok])
            P.tt(g1, ys, ys, ALU.mult)
            P.ts(g1, g1, 0.044715, ALU.mult, 1.0, ALU.add)
            P.tt(g1, g1, ys, ALU.mult)
            P.ts(g1, g1, -50.0, ALU.max)
            P.act(g2, g1, AF.Exp, scale=-GC)
            P.ts(g2, g2, 1.0, ALU.add)
            P.op("vector", "reciprocal", out=g2, in_=g2)
            P.tt(HT.p(kcc)[:, kcc, 0:ntok], ys, g2, ALU.mult)
        zb = [PS[0], PS[1], PS[2], PS[3]]
        for oc in range(4):
            for kcc in range(4):
                P.mm(zb[oc][:, 0:ntok], wglu[:, kcc, oc * 128:(oc + 1) * 128], HT.p(kcc)[:, kcc, 0:ntok], start=(kcc == 0), stop=(kcc == 3))
        ya = tv(1024, [128, 4, ntok], BF16)
        for oc in range(4):
            e = tv(2048 + (oc % 2) * 512, [128, ntok])
            P.act(e, zb[oc][:, 0:ntok], AF.Exp, scale=-1.0, bias=sc[:, 4 + oc:5 + oc])
            P.ts(e, e, 1.0, ALU.add)
            P.op("vector", "reciprocal", out=e, in_=e)
            P.tt(ya[:, oc, :], HT.p(oc)[:, oc, 0:ntok], e, ALU.mult)
        for oc in range(4):
            P.copy(HT.p(oc)[:, oc, 0:ntok], ya[:, oc, :], eng="scalar")

    def conv_mixer(nseq, T, carry, ntok):
        W = T + 2
        for kcc in range(4):
            ext = tv(0, [128, nseq, W])
            a = tv(1100 + (kcc % 2) * 600, [128, nseq, T])
            gb = HT.p(4 + kcc)[:, 4 + kcc, 0:ntok].rearrange("p (s t) -> p s t", s=nseq)
            gc = HT.p(8 + kcc)[:, 8 + kcc, 0:ntok].rearrange("p (s t) -> p s t", s=nseq)
            vv = HT.p(12 + kcc)[:, 12 + kcc, 0:ntok].rearrange("p (s t) -> p s t", s=nseq)
            P.copy(ext[:, :, 0:2], carry[:, kcc, :, :], eng="scalar")
            P.tt(ext[:, :, 2:W], gc, vv, ALU.mult)
            P.copy(carry[:, kcc, :, :], ext[:, :, T:W], eng="scalar")
            P.ts(a, ext[:, :, 0:T], sc[:, 8 + kcc:9 + kcc], ALU.mult, sc[:, 20 + kcc:21 + kcc], ALU.add)
            P.stt(a, ext[:, :, 1:T + 1], sc[:, 12 + kcc:13 + kcc], a, ALU.mult, ALU.add)
            P.stt(a, ext[:, :, 2:T + 2], sc[:, 16 + kcc:17 + kcc], a, ALU.mult, ALU.add)
            P.tt(gb, gb, a, ALU.mult)

    def pool_mixer(nseq, T, carry, ntok, first_chunk):
        W = T + 15
        wins_ = (2, 4, 8, 16)
        for kcc in range(4):
            w = wins_[kcc]
            e0 = tv(0, [128, nseq, W]); e1 = tv(2200, [128, nseq, W])
            up = HT.p(16 + kcc)[:, 16 + kcc, 0:ntok].rearrange("p (s t) -> p s t", s=nseq)
            P.copy(e0[:, :, 0:15], carry[:, kcc, :, :], eng="scalar")
            P.copy(e0[:, :, 15:W], up)
            P.copy(carry[:, kcc, :, :], e0[:, :, T:W], eng="scalar")
            src, dst = e0, e1
            sh = 1
            while sh < w:
                P.tt(dst[:, :, sh:W], src[:, :, sh:W], src[:, :, 0:W - sh], ALU.add)
                if sh > 0:
                    P.copy(dst[:, :, 0:sh], src[:, :, 0:sh], eng="scalar")
                src, dst = dst, src
                sh *= 2
            pl = tv(4400, [128, nseq, T], BF16)
            P.stt(pl, src[:, :, 15:W], 1.0 / w, up, ALU.mult, ALU.subtract)
            if first_chunk:
                rc = tv(5000, [128, 16]); t16 = tv(5020, [128, 16])
                P.dma(rc, c_rcnt0[:, kcc, :])
                P.tt(t16, src[:, 0, 15:31], rc, ALU.mult)
                P.tt(pl[:, 0, 0:16], t16, up[:, 0, 0:16], ALU.subtract)
            pb = bank()
            P.mm(pb[:, 0:ntok], poolw[:, kcc, :], pl.rearrange("p s t -> p (s t)"))
            P.ts(HT.p(16 + kcc)[:, 16 + kcc, 0:ntok], pb[:, 0:ntok], sc[:, 24 + kcc:25 + kcc], ALU.mult)

    def nsa_prompt(l, c):
        ydT_done = []
        for i in range(NT):
            T = 4 * c + i
            tm = tv(0, [128, 3, 64]); cbT = tv(192, [64, 128])
            P.dma(tm, c_tmask[T]); P.dma(cbT, c_cmpbT[T])
            cbT4 = tv(320, [128, 4, 128], BF16)
            P.memset(cbT4[64:128], NEG)
            P.copy(cbT4[0:64], cbT.map(lambda a: a.unsqueeze(1).broadcast_to([64, 4, 128])))
            yd = tv(600, [128, 8, 64])
            qc = tv(4500, [128, 4, 128], BF16)
            P.copy(qc, HT.p(20, 21, 22, 23)[:, 20:24, i * 128:(i + 1) * 128], eng="scalar")
            for g in range(2):
                gp = slice(g * 64, (g + 1) * 64)
                qg = qc[gp].rearrange("p r t -> p (r t)")
                sb = PS[5]
                for r in range(4):
                    P.mm(sb[:, r * 64:(r + 1) * 64], HT.p(20 + r)[gp, 20 + r, i * 128:(i + 1) * 128], kcT[gp, :])
                s_ = tv(1200, [128, 4, 64]); sm = tv(1460, [128, 4]); imp = tv(1470, [128, 64]); m8 = tv(1540, [128, 16]); sc2 = tv(1560, [128, 64])
                P.stt(s_, sb[:, 0:256].rearrange("p (r j) -> p r j", r=4), 0.125, tm[:, 0:1, :].map(lambda a: a.broadcast_to([128, 4, 64])), ALU.mult, ALU.add)
                P.act(s_, s_, AF.Exp)
                P.op("vector", "tensor_reduce", out=sm, in_=s_, axis=AX.X, op=ALU.add)
                P.ts(sm, sm, 1e-30, ALU.max)
                P.op("vector", "reciprocal", out=sm, in_=sm)
                P.tt(s_, s_, sm.map(lambda a: a.unsqueeze(2).broadcast_to([128, 4, 64])), ALU.mult)
                P.op("vector", "tensor_reduce", out=imp, in_=s_.rearrange("p r j -> p j r"), axis=AX.X, op=ALU.add)
                P.tt(imp, imp, tm[:, 1, :], ALU.mult)
                P.tt(imp, imp, tm[:, 2, :], ALU.add)
                P.op("vector", "max", out=m8[:, 0:8], in_=imp)
                P.op("vector", "match_replace", out=sc2, in_to_replace=m8[:, 0:8], in_values=imp, imm_value=-1e30)
                P.op("vector", "max", out=m8[:, 8:16], in_=sc2)
                P.ts(sc2, imp, m8[:, 15:16], ALU.is_lt, NEG, ALU.mult)
                tb_ = PS[5]
                P.op("tensor", "transpose", out=tb_[0:64, 256:384], in_=sc2, identity=identf.all())
                sbT4 = tv(1700, [128, 4, 128], BF16)
                P.memset(sbT4[64:128], 0.0)
                P.copy(sbT4[0:64], tb_[0:64, 256:384].map(lambda a: a.unsqueeze(1).broadcast_to([64, 4, 128])))
                qz = tv(4760, [128, 512], BF16)
                P.memset(qz, 0.0)
                P.copy(qz[gp], qg, eng="scalar")
                qfull = qc.rearrange("p r t -> p (r t)")
                if NSA_LEVEL < 2:
                    continue
                Osb = [tv(5200 + bi_ * 260, [128, 4, 65]) for bi_ in range(3)]
                def branch(bi_, items):
                    n = len(items)
                    for idx, (lhs_list, vrhs, kp) in enumerate(items):
                        stb = PS[idx % 2]
                        for k_, (lt, rh) in enumerate(lhs_list):
                            P.mm(stb[0:kp, :], lt, rh, start=(k_ == 0), stop=(k_ == len(lhs_list) - 1))
                        pt = tv(2300 + (idx % 3) * 256, [128, 512], BF16)
                        P.act(pt[0:kp, :], stb[0:kp, :], AF.Exp, scale=0.125)
                        for r in range(4):
                            P.mm(PS[2 + r][:, 0:65], pt[0:kp, r * 128:(r + 1) * 128], vrhs, start=(idx == 0), stop=(idx == n - 1))
                    for r in range(4):
                        P.copy(Osb[bi_][:, r, :], PS[2 + r][:, 0:65], eng="scalar")
                q2 = qg
                branch(0, [([(kcTz[:, g, :], qfull), (identb.all(), cbT4.rearrange('p r t -> p (r t)'))], vc1[:, g, 0:65], 128)])
                if NSA_LEVEL < 3:
                    continue
                items = []
                for kc_ in range(T + 1):
                    ll = [(KselT.p(kc_)[:, kc_ * 128:(kc_ + 1) * 128], qz), (ebig[:, kc_ * 128:(kc_ + 1) * 128], sbT4.rearrange('p r t -> p (r t)'))]
                    if kc_ == T:
                        ll.append((identb.all(), caus4.all().rearrange('p r t -> p (r t)')))
                    items.append((ll, Vsel1.p(kc_)[:, kc_, g, 0:65], 128))
                branch(1, items)
                if NSA_LEVEL < 4:
                    continue
                items = []
                for kc_ in range(max(0, T - 4), T + 1):
                    ll = [(KwinT.p(kc_ % 8)[:, kc_ % 8, :], qz)]
                    if kc_ == T:
                        ll.append((identb.all(), caus4.all().rearrange('p r t -> p (r t)')))
                    elif kc_ == T - 4:
                        ll.append((identb.all(), anti4.all().rearrange('p r t -> p (r t)')))
                    items.append((ll, Vwin1.p(kc_ % 8)[:, kc_ % 8, g, 0:65], 128))
                branch(2, items)
                if NSA_LEVEL < 5:
                    continue
                for bi in range(3):
                    O3 = Osb[bi]
                    dn = tv(3100, [128, 4]); tmpo = tv(3110, [128, 4, 64])
                    P.ts(dn, O3[:, :, 64], 1e-30, ALU.max)
                    P.op("vector", "reciprocal", out=dn, in_=dn)
                    P.tt(dn, dn, gate[:, i, :].rearrange("p (h b) -> p h b", b=3)[:, 4 * g:4 * g + 4, bi], ALU.mult)
                    dst = yd[:, 4 * g:4 * g + 4, :]
                    if bi == 0:
                        P.tt(dst, O3[:, :, 0:64], dn.map(lambda a: a.unsqueeze(2).broadcast_to([128, 4, 64])), ALU.mult)
                    else:
                        P.tt(tmpo, O3[:, :, 0:64], dn.map(lambda a: a.unsqueeze(2).broadcast_to([128, 4, 64])), ALU.mult)
                        P.tt(dst, dst, tmpo, ALU.add)
            if NSA_LEVEL < 5:
                continue
            pb = PS[7]
            ydf = yd.rearrange("p h d -> p (h d)")
            for k in range(4):
                P.op("tensor", "transpose", out=pb[:, k * 128:(k + 1) * 128], in_=ydf[:, k * 128:(k + 1) * 128], identity=identf.all())
            ydT_done.append((i, pb))
            P.copy(tv(3400 + i * 256, [128, 4, 128], BF16), pb.all().rearrange("p (k t) -> p k t", k=4), eng="scalar")
        if NSA_LEVEL < 5:
            return
        for i in range(NT):
            P.copy(HT.p(20, 21, 22, 23)[:, 20:24, i * 128:(i + 1) * 128], tv(3400 + i * 256, [128, 4, 128], BF16))

    def prompt_chunk(l, c):
        t0 = c * NCH
        src = x_p if l == 0 else x1d
        P.dma(ACC.all(), src[t0:t0 + NCH, :].rearrange("(i p) f -> p i f", p=128))
        transpose_to_T(XT, NT)
        kvtok = tv(2900, [128, NT, 792])
        def fm_block(col0, chunk_cols, dsts, tok=False, tok_cols=None, tok_off=0):
            banks = [bank() for _ in chunk_cols]
            tbanks = [bank() for _ in range(NT)] if tok else []
            ncols = tok_cols if tok else 512
            for half in range(2):
                wv = wload(w_in, l, half * 1024, 8, col0, ncols)
                for ci, mk in enumerate(chunk_cols):
                    for k in range(8):
                        kk = half * 8 + k
                        P.mm(banks[ci].all(), mk(wv, k), XT.p(kk)[:, kk, :], start=(kk == 0), stop=(kk == 15))
                for i in range(len(tbanks)):
                    for k in range(8):
                        kk = half * 8 + k
                        P.mm(tbanks[i][:, 0:ncols], XT.p(kk)[:, kk, i * 128:(i + 1) * 128], wv[:, k, :], start=(kk == 0), stop=(kk == 15))
            for ci, dv in enumerate(dsts):
                P.copy(dv, banks[ci].all(), eng=evac_eng())
            for i in range(len(tbanks)):
                P.copy(kvtok[:, i, tok_off:tok_off + ncols], tbanks[i][:, 0:ncols], eng=evac_eng())
        std = lambda j: (lambda wv, k: wv[:, k, j * 128:(j + 1) * 128])
        for blk in range(5):
            fm_block(blk * 512, [std(j) for j in range(4)], [HT.p(blk * 4 + j)[:, blk * 4 + j, :] for j in range(4)])
        fm_block(2560, [std(r) for r in range(4)], [HT.p(20 + r)[:, 20 + r, :] for r in range(4)])
        Tt = [4 * c + i for i in range(NT)]
        fm_block(3072, [std(0), std(1), std(2)],
                 [HT.p(24)[:, 24, :], HT.p(25)[:, 25, :], KselT.p(*Tt)[:, t0:t0 + NCH]], tok=True, tok_cols=512, tok_off=0)
        ws_ = (4 * c) % 8
        fm_block(3584, [std(0)], [KwinT.p(*range(ws_, ws_ + 4))[:, ws_:ws_ + 4, :].rearrange("p a t -> p (a t)")],
                 tok=True, tok_cols=280, tok_off=512)
        P.dma(kvp[l, t0:t0 + NCH, :].rearrange("(i p) f -> p i f", p=128), kvtok[:, :, 0:512], is_output=True)
        if c == n_chunks - 1:
            P.dma(winp[l].rearrange("(i p) f -> p i f", p=128), kvtok[:, :, 512:768], is_output=True)
        P.copy(Vsel1.p(*Tt)[:, 4 * c:4 * c + 4, :, 0:64], kvtok[:, :, 384:512].rearrange("p i (g d) -> p i g d", g=2))
        P.copy(Vwin1.p(*range(ws_, ws_ + 4))[:, ws_:ws_ + 4, :, 0:64], kvtok[:, :, 640:768].rearrange("p i (g d) -> p i g d", g=2))
        P.act(gate.all(), kvtok[:, :, 768:792], AF.Exp, scale=-1.0)
        P.ts(gate.all(), gate.all(), 1.0, ALU.add)
        P.op("vector", "reciprocal", out=gate.all(), in_=gate.all())
        for srcc, wT, dstT in ((24, wckT, kcT), (25, wcvT, vcT)):
            tc_ = tv(0, [128, 8, 64])
            P.tt(tc_, HT.p(srcc)[:, srcc, :].rearrange("p (b k) -> p b k", k=64), wT.all().map(lambda a: a.unsqueeze(1).broadcast_to([128, 8, 64])), ALU.mult)
            P.op("vector", "tensor_reduce", out=dstT[:, 8 * c:8 * c + 8], in_=tc_, axis=AX.X, op=ALU.add)
        pbv = PS[6]
        pv16 = pbv.all().bitcast(BF16)
        P.op("tensor", "transpose", out=pv16[0:64, 0:128], in_=vcT.all(), identity=identb.all())
        P.copy(vc1[0:64, :, 0:64], pv16[0:64, 0:128].rearrange("p (g d) -> p g d", g=2))
        P.memset(kcTz.all(), 0.0)
        for g_ in range(2):
            P.copy(kcTz[g_ * 64:(g_ + 1) * 64, g_, 0:64], kcT[g_ * 64:(g_ + 1) * 64, :])
        if "s5" in parts:
            s5_mixer(l, NCH // LSEG, LSEG, True, hst.all().rearrange("p m (s r) -> p m s r", s=1), NCH)
        if "conv" in parts:
            conv_mixer(1, NCH, convc.all().rearrange("p k (s j) -> p k s j", s=1), NCH)
        if "pool" in parts:
            pool_mixer(1, NCH, poolc.all().rearrange("p k (s j) -> p k s j", s=1), NCH, first_chunk=(c == 0))
        if "nsa" in parts:
            nsa_prompt(l, c)
        if c == n_chunks - 1:
            P.dma(sre_p[l].rearrange("(m two) n -> two n m", two=2)[0], hst[0:64, :, 0], is_output=True, allow_slow_non_contiguous=True)
            P.dma(sre_p[l].rearrange("(m two) n -> two n m", two=2)[1], hst[64:128, :, 0], is_output=True, allow_slow_non_contiguous=True)
            P.dma(sim_p[l].rearrange("(m two) n -> two n m", two=2)[0], hst[0:64, :, 1], is_output=True, allow_slow_non_contiguous=True)
            P.dma(sim_p[l].rearrange("(m two) n -> two n m", two=2)[1], hst[64:128, :, 1], is_output=True, allow_slow_non_contiguous=True)
            for j in range(2):
                P.dma(conv_p[l, j].rearrange("(k p) -> p k", p=128), convc[:, :, j], is_output=True, allow_slow_non_contiguous=True)
            for j in range(15):
                P.dma(pool_p[l, j].rearrange("(k p) -> p k", p=128), poolc[:, :, j], is_output=True, allow_slow_non_contiguous=True)
        if DEBUG:
            P.dma(d["dbg"][c][:, 0:26], HT.all()[:, 0:26, :], eng="gpsimd", is_output=True)
        if "ffn" not in parts:
            return
        dense_tail(l, NT, 128, NCH)
        dst = x1d if l < n_layers - 1 else y_p
        P.dma(dst[t0:t0 + NCH, :].rearrange("(i p) f -> p i f", p=128), ACC.all(), is_output=(l == n_layers - 1))

    def dense_tail(l, ntl, rows, ntok):
        ycat = [0, 1, 2, 3, 4, 5, 6, 7, 16, 17, 18, 19, 20, 21, 22, 23]
        tsz = lambda i: slice(i * 128, i * 128 + rows)
        for j in range(4):
            banks = [bank() for _ in range(ntl)]
            for half in range(2):
                wv = wload(w_out, l, half * 1024, 8, j * 512, 512)
                for i in range(ntl):
                    for k in range(8):
                        kk = half * 8 + k
                        P.mm(banks[i][0:rows, :], HT.p(ycat[kk])[:, ycat[kk], tsz(i)], wv[:, k, :], start=(kk == 0), stop=(kk == 15))
            for i in range(ntl):
                av = ACC.p(i)[0:rows, i, j * 512:(j + 1) * 512]
                P.stt(av, av, ALPHA_, banks[i][0:rows, :], ALU.mult, ALU.add)
        gb = load_gb(l, ln1g, ln1b)
        for i in range(ntl):
            layer_norm(i, gb, rows)
        transpose_to_T(XT, ntl, rows)
        for i in range(ntl):
            P.ts(ACC.p(i)[0:rows, i, :], ACC.p(i)[0:rows, i, :], ALPHA_, ALU.mult)
        for fg in range(16):
            ub = [bank() for _ in range(4)]
            for half in range(2):
                wv = wload(w_up, l, half * 1024, 8, fg * 512, 512)
                for j in range(4):
                    for k in range(8):
                        kk = half * 8 + k
                        P.mm(ub[j][:, 0:ntok], wv[:, k, j * 128:(j + 1) * 128], XT.p(kk)[:, kk, 0:ntok], start=(kk == 0), stop=(kk == 15))
            hid = tv(4096 + (fg % 2) * 1024, [128, 4, NCH], BF16)
            for j in range(4):
                r_ = tv((j % 2) * 256, [128, NCH], BF16)
                P.act(r_[:, 0:ntok], ub[j][:, 0:ntok], AF.Relu)
                P.tt(hid[:, j, 0:ntok], r_[:, 0:ntok], r_[:, 0:ntok], ALU.mult)
            for half in range(2):
                wv = wload(w_down, l, fg * 512, 4, half * 1024, 1024)
                for i in range(ntl):
                    for jj in range(2):
                        pb = bank()
                        for k in range(4):
                            P.mm(pb[0:rows, :], hid[:, k, tsz(i)], wv[:, k, jj * 512:(jj + 1) * 512], start=(k == 0), stop=(k == 3))
                        av = ACC.p(i)[0:rows, i, (half * 2 + jj) * 512:(half * 2 + jj + 1) * 512]
                        P.tt(av, av, pb[0:rows, :], ALU.add)
        gb = load_gb(l, ln2g, ln2b)
        for i in range(ntl):
            layer_norm(i, gb, rows)

    sst = P.sbuf("sst", [128, 16, NB, 2], F32)
    convc_s = P.sbuf("convc_s", [128, 4, NB, 2], F32); poolc_s = P.sbuf("poolc_s", [128, 4, NB, 15], F32)
    xs1d = P.dram("xs1d", [NS, D], F32, IN)
    c_iota = d["c_iota"]; c_ibig = d["c_ibig"]; c_nb = d["c_nb"]; c_wb0 = d["c_wb0"]; c_selb = d["c_selb"]; c_rm = d["c_rm"]
    ibig = P.sbuf("ibig", [128, 254], BF16); iota = P.sbuf("iota", [128, 1], F32)
    P.dma(ibig.all(), c_ibig.all(), eng="gpsimd"); P.dma(iota.all(), c_iota.all())

    def nsa_sample(l, kvtok):
        ebs = XT.all().rearrange("p a n -> p (a n)")
        P.dma(ebs, c_ebig.all(), eng="gpsimd")
        wtok = tv(4888, [128, 256])
        for half in range(2):
            pr = slice(half * 64, half * 64 + 64)
            for g in range(2):
                P.dma(wtok[pr, g * 64:(g + 1) * 64], wck[l]); P.dma(wtok[pr, 128 + g * 64:128 + (g + 1) * 64], wcv[l])
        qcs = tv(5144, [128, NB, 16], BF16)
        P.copy(qcs.rearrange("p b (r t) -> p r b t", r=4), HT.p(20, 21, 22, 23)[:, 20:24, 0:NS].rearrange("p r (b t) -> p r b t", t=4))
        qzs = [tv(5816, [128, NB, 16], BF16), tv(5944, [128, NB, 16], BF16)]
        for g_ in range(2):
            P.memset(qzs[g_], 0.0)
            P.copy(qzs[g_][g_ * 64:(g_ + 1) * 64], qcs[g_ * 64:(g_ + 1) * 64], eng="scalar")
        nbias = tv(5272, [128, NB, 16], BF16); wb0 = tv(5400, [128, 16], BF16); Selb = tv(5552, [NS, NB, 16]); rm = tv(5808, [16, 6])
        P.memset(nbias, 0.0); P.dma(nbias[0:NS], c_nb.all(), eng="gpsimd"); P.dma(wb0, c_wb0.all(), eng="gpsimd"); P.dma(Selb, c_selb.all()); P.dma(rm, c_rm.all())
        Vnew1 = tv(5408, [128, 2, 2, 72], BF16)
        P.memset(Vnew1, 1.0)
        P.copy(Vnew1[0:NS, 0, :, 0:64], kvtok[0:NS, 0, 384:512].rearrange("p (g d) -> p g d", g=2))
        P.copy(Vnew1[0:NS, 1, :, 0:64], kvtok[0:NS, 0, 640:768].rearrange("p (g d) -> p g d", g=2))
        PTn = tv(4436, [128, 16], BF16)
        P.memset(PTn, 0.0)
        table = cache.all().rearrange("l r (h c) -> (l r h) c", h=2)
        for b in range(NB):
            ptb = tv(3168, [128, 64]).bitcast(I32); idf = tv(3232, [128, 64]); idxA = tv(3296, [128, 64]).bitcast(I32); idxB = tv(3360, [128, 64]).bitcast(I32)
            P.dma(ptb, ptab[b:b + 1, :].map(lambda a: a.broadcast_to([128, 64])))
            P.copy(idf, ptb)
            P.ts(idf, idf, 128.0, ALU.mult, iota[:, 0:1], ALU.add)
            if l > 0:
                P.ts(idf, idf, float(l * CACHE_ROWS), ALU.add)
            P.ts(idf, idf, 2.0, ALU.mult)
            P.copy(idxA, idf)
            P.ts(idf, idf, 1.0, ALU.add)
            P.copy(idxB, idf)
            def gather(dst, idx_col):
                def fn(e, o=dst.ap, i=table.ap, ia=idx_col.ap):
                    return e.indirect_dma_start(out=o, out_offset=None, in_=i, in_offset=bass.IndirectOffsetOnAxis(ap=ia, axis=0))
                P.dma_custom(fn, reads=[idx_col, table], writes=[dst])
            kvc = PS[7]
            for grp in range(16):
                pg = tv((grp % 2) * 1024, [128, 4, 256])
                for j in range(4):
                    gather(pg[:, j, :], idxA[:, grp * 4 + j:grp * 4 + j + 1])
                kw = tv(2048, [128, 4, 256], BF16)
                P.tt(kw, pg, wtok.map(lambda a: a.unsqueeze(1).broadcast_to([128, 4, 256])), ALU.mult)
                for j in range(4):
                    pgi = grp * 4 + j
                    P.mm(kvc[:, 0:256], ibig[:, 126 - 2 * pgi:254 - 2 * pgi], kw[:, j, :], start=(pgi == 0), stop=(pgi == 63))
            kc_tok = tv(3424, [128, 128], BF16); kcT_s = tv(3488, [128, 128], BF16); vc1_s = tv(3552, [128, 2, 72], BF16)
            P.copy(kc_tok, kvc[:, 0:128], eng="scalar")
            P.memset(vc1_s, 1.0)
            P.copy(vc1_s[:, :, 0:64], kvc[:, 128:256].rearrange("p (g d) -> p g d", g=2))
            tp = PS[6].all().bitcast(BF16)
            P.op("tensor", "transpose", out=tp[:, 0:128], in_=kc_tok, identity=identb.all())
            P.copy(kcT_s, tp[:, 0:128], eng="scalar")
            selbT = tv(4412, [128, 2, 16], BF16)
            Osb = tv(2048, [16, 3, 2, 65])
            for g in range(2):
                gp = slice(g * 64, (g + 1) * 64)
                qb = qcs[gp, b, :]
                sb = PS[5]
                for r in range(4):
                    P.mm(sb[0:4, r * 128:(r + 1) * 128], qcs[gp, b, r * 4:(r + 1) * 4], kcT_s[gp, :])
                s_ = tv(3624, [4, 4, 128]); sm = tv(4136, [4, 4]); imp = tv(4140, [4, 128]); m8 = tv(4268, [4, 16]); sc2 = tv(4284, [4, 128])
                P.act(s_, sb[0:4, :].rearrange("p (r j) -> p r j", r=4), AF.Exp, scale=0.125)
                P.op("vector", "tensor_reduce", out=sm, in_=s_, axis=AX.X, op=ALU.add)
                P.op("vector", "reciprocal", out=sm, in_=sm)
                P.tt(s_, s_, sm.map(lambda a: a.unsqueeze(2).broadcast_to([4, 4, 128])), ALU.mult)
                P.op("vector", "tensor_reduce", out=imp, in_=s_.rearrange("p r j -> p j r"), axis=AX.X, op=ALU.add)
                P.memset(imp[:, 0:1], 5.0); P.memset(imp[:, 127:128], 5.0)
                P.op("vector", "max", out=m8[:, 0:8], in_=imp)
                P.op("vector", "match_replace", out=sc2, in_to_replace=m8[:, 0:8], in_values=imp, imm_value=-1e30)
                P.op("vector", "max", out=m8[:, 8:16], in_=sc2)
                P.ts(sc2, imp, m8[:, 14:15], ALU.is_lt, NEG, ALU.mult)
                tq = PS[5]
                P.op("tensor", "transpose", out=tq[:, 508:512], in_=sc2, identity=identf[0:4, 0:4])
                P.copy(selbT[:, g, :].rearrange("p (r t) -> p r t", r=4), tq[:, 508:512].map(lambda a: a.unsqueeze(1).broadcast_to([128, 4, 4])))
                stc = PS[0]
                P.mm(stc[:, 0:16], kcT_s[gp, :], qb)
                PcT = tv(4428, [128, 16], BF16)
                P.act(PcT, stc[:, 0:16], AF.Exp, scale=0.125)
                oc_ = PS[2]
                P.mm(oc_[0:16, 0:65], PcT, vc1_s[:, g, 0:65])
                P.copy(Osb[:, 0, g, :], oc_[0:16, 0:65], eng="scalar")
            osel = [PS[3], PS[4]]
            V1g = tv(2816, [128, 4, 2, 72], BF16)
            P.memset(V1g, 1.0)
            for grp in range(16):
                pg = tv((grp % 2) * 1024, [128, 4, 256])
                for j in range(4):
                    gather(pg[:, j, :], idxB[:, grp * 4 + j:grp * 4 + j + 1])
                tb_ = PS[grp % 2]
                for j in range(4):
                    P.op("tensor", "transpose", out=tb_[:, j * 128:(j + 1) * 128], in_=pg[:, j, 0:128], identity=identf.all())
                KT = tv(2560, [128, 512], BF16)
                P.copy(KT, tb_.all(), eng="scalar")
                P.copy(V1g[:, :, :, 0:64], pg[:, :, 128:256].rearrange("p j (g d) -> p j g d", g=2))
                stb = PS[5]
                for j in range(4):
                    pgi = grp * 4 + j
                    for g in range(2):
                        gp = slice(g * 64, (g + 1) * 64)
                        oc = (j * 2 + g) * 16
                        P.mm(stb[:, oc:oc + 16], KT[:, j * 128:(j + 1) * 128], qzs[g][:, b, :], start=True, stop=False)
                        P.mm(stb[:, oc:oc + 16], ebs[:, pgi * 128:(pgi + 1) * 128], selbT[:, g, :], start=False, stop=True)
                PT = tv(3104, [128, 128], BF16)
                P.act(PT, stb[:, 0:128], AF.Exp, scale=0.125)
                for j in range(4):
                    for g in range(2):
                        oc = (j * 2 + g) * 16
                        P.mm(osel[g][0:16, 0:65], PT[:, oc:oc + 16], V1g[:, j, g, 0:65], start=(grp == 0 and j == 0), stop=False)
            owin = [PS[6], PS[7]]
            wbuf = tv(0, [128, 4, 256])
            P.dma(wbuf, cwin[l, b].rearrange("(i p) f -> p i f", p=128))
            tbw = PS[1]
            for i in range(4):
                P.op("tensor", "transpose", out=tbw[:, i * 128:(i + 1) * 128], in_=wbuf[:, i, 0:128], identity=identf.all())
            KTw = tv(2560, [128, 512], BF16)
            P.copy(KTw, tbw.all(), eng="scalar")
            P.copy(V1g[:, :, :, 0:64], wbuf[:, :, 128:256].rearrange("p j (g d) -> p j g d", g=2))
            stw = PS[5]
            for i in range(4):
                for g in range(2):
                    gp = slice(g * 64, (g + 1) * 64)
                    oc = (i * 2 + g) * 16
                    P.mm(stw[:, oc:oc + 16], KTw[:, i * 128:(i + 1) * 128], qzs[g][:, b, :], start=True, stop=(i != 0))
                    if i == 0:
                        P.mm(stw[:, oc:oc + 16], identb.all(), wb0, start=False, stop=True)
            PTw = tv(3104, [128, 128], BF16)
            P.act(PTw, stw[:, 0:128], AF.Exp, scale=0.125)
            for i in range(4):
                for g in range(2):
                    oc = (i * 2 + g) * 16
                    P.mm(owin[g][0:16, 0:65], PTw[:, oc:oc + 16], V1g[:, i, g, 0:65], start=(i == 0), stop=False)
            for bi, (kch, og) in enumerate(((26, osel), (27, owin))):
                stn = PS[0]
                for g in range(2):
                    gp = slice(g * 64, (g + 1) * 64)
                    P.mm(stn[0:NS, g * 16:(g + 1) * 16], HT.p(kch)[:, kch, 0:NS], qzs[g][:, b, :], start=True, stop=False)
                    P.mm(stn[0:NS, g * 16:(g + 1) * 16], identb[:, 0:NS], nbias[:, b, :], start=False, stop=True)
                for g in range(2):
                    P.act(PTn[0:NS], stn[0:NS, g * 16:(g + 1) * 16], AF.Exp, scale=0.125)
                    P.mm(og[g][0:16, 0:65], PTn, Vnew1[:, bi, g, 0:65], start=False, stop=True)
                    P.copy(Osb[:, 1 + bi, g, :], og[g][0:16, 0:65], eng="scalar")
            G16 = tv(4444, [16, 24]); gm = tv(4468, [16, 24]); gsel = tv(4492, [16, 6]); dn = tv(4498, [16, 2])
            y16 = tv(4504, [16, 2, 64]); tmpy = tv(4632, [16, 2, 64]); Y2 = tv(4760, [16, 128])
            gpb = PS[1]
            P.mm(gpb[0:16, 0:24], Selb[:, b, :], gate[0:NS, 0, :])
            P.copy(G16, gpb[0:16, 0:24], eng="scalar")
            P.tt(gm.rearrange("p (g r b) -> p g r b", g=2, r=4), G16.rearrange("p (g r b) -> p g r b", g=2, r=4),
                 rm[:, 2:6].map(lambda a: a.unsqueeze(1).unsqueeze(3).broadcast_to([16, 2, 4, 3])), ALU.mult)
            P.op("vector", "tensor_reduce", out=gsel.rearrange("p (g b) -> p g b", g=2), in_=gm.rearrange("p (g r b) -> p g b r", g=2, r=4), axis=AX.X, op=ALU.add)
            for bi in range(3):
                P.ts(dn, Osb[:, bi, :, 64], 1e-30, ALU.max)
                P.op("vector", "reciprocal", out=dn, in_=dn)
                P.tt(dn, dn, gsel.rearrange("p (g b) -> p g b", g=2)[:, :, bi], ALU.mult)
                dst = y16 if bi == 0 else tmpy
                P.tt(dst, Osb[:, bi, :, 0:64], dn.map(lambda a: a.unsqueeze(2).broadcast_to([16, 2, 64])), ALU.mult)
                if bi > 0:
                    P.tt(y16, y16, tmpy, ALU.add)
            for g in range(2):
                P.ts(Y2[:, 0:64], y16[:, g, :], rm[:, 0:1], ALU.mult)
                P.ts(Y2[:, 64:128], y16[:, g, :], rm[:, 1:2], ALU.mult)
                tpy_ps = PS[0]
                P.op("tensor", "transpose", out=tpy_ps[:, 0:16], in_=Y2, identity=identf[0:16, 0:16])
                tpy = tv(6080, [128, 16])
                P.copy(tpy, tpy_ps[:, 0:16], eng="scalar")
                for rr_ in range(2):
                    ch = 20 + 2 * g + rr_
                    P.tt(HT.p(ch)[:, ch, 4 * b:4 * b + 4], tpy[:, (2 * rr_) * 4:(2 * rr_) * 4 + 4], tpy[:, (2 * rr_ + 1) * 4:(2 * rr_ + 1) * 4 + 4], ALU.add)

    def sample_layer(l):
        src = x_s if l == 0 else xs1d
        P.dma(ACC.p(0)[0:NS, 0, :], src.all())
        transpose_to_T(XT, 1, NS)
        kvtok = tv(2900, [128, NT, 792])
        def fm_s(col0, ncols, chunk_ids, dsts, tok=False, tok_off=0):
            banks = [bank() for _ in chunk_ids]
            tb = bank() if tok else None
            for half in range(2):
                wv = wload(w_in, l, half * 1024, 8, col0, ncols)
                for ci, j in enumerate(chunk_ids):
                    for k in range(8):
                        kk = half * 8 + k
                        P.mm(banks[ci][:, 0:NS], wv[:, k, j * 128:(j + 1) * 128], XT.p(kk)[:, kk, 0:NS], start=(kk == 0), stop=(kk == 15))
                if tok:
                    for k in range(8):
                        kk = half * 8 + k
                        P.mm(tb[0:NS, 0:ncols], XT.p(kk)[:, kk, 0:NS], wv[:, k, :], start=(kk == 0), stop=(kk == 15))
            for ci, dv in enumerate(dsts):
                P.copy(dv, banks[ci][:, 0:NS], eng=evac_eng())
            if tok:
                P.copy(kvtok[0:NS, 0, tok_off:tok_off + ncols], tb[0:NS, 0:ncols], eng=evac_eng())
        for blk in range(6):
            fm_s(blk * 512, 512, [0, 1, 2, 3], [HT.p(blk * 4 + j)[:, blk * 4 + j, 0:NS] for j in range(4)])
        fm_s(3072, 512, [0, 1, 2], [HT.p(24 + j)[:, 24 + j, 0:NS] for j in range(3)], tok=True, tok_off=0)
        fm_s(3584, 280, [0], [HT.p(27)[:, 27, 0:NS]], tok=True, tok_off=512)
        P.dma(kvs[l], kvtok[0:NS, 0, 0:512], is_output=True)
        P.dma(wins[l][:, 0:508, :], cwin[l][:, 4:512, :], is_output=True)
        for b in range(NB):
            P.dma(wins[l, b, 508:512, :], kvtok[4 * b:4 * b + 4, 0, 512:768], is_output=True)
        P.act(gate[0:NS, 0, :], kvtok[0:NS, 0, 768:792], AF.Exp, scale=-1.0)
        P.ts(gate[0:NS, 0, :], gate[0:NS, 0, :], 1.0, ALU.add)
        P.op("vector", "reciprocal", out=gate[0:NS, 0, :], in_=gate[0:NS, 0, :])
        for b in range(NB):
            for two in range(2):
                pr = slice(two * 64, (two + 1) * 64)
                P.dma(sst[pr, :, b, 0], st_re[l, b].rearrange("(m two) n -> two n m", two=2)[two], allow_slow_non_contiguous=True)
                P.dma(sst[pr, :, b, 1], st_im[l, b].rearrange("(m two) n -> two n m", two=2)[two], allow_slow_non_contiguous=True)
            for kcc in range(4):
                P.dma(convc_s[:, kcc, b, :], st_conv[l, b][:, kcc * 128:(kcc + 1) * 128].rearrange("j p -> p j"), allow_slow_non_contiguous=True)
                P.dma(poolc_s[:, kcc, b, :], st_pool[l, b][:, kcc * 128:(kcc + 1) * 128].rearrange("j p -> p j"), allow_slow_non_contiguous=True)
        if "s5" in parts:
            s5_mixer(l, NB, 4, False, sst.all(), NS)
        if "conv" in parts:
            conv_mixer(NB, 4, convc_s.all(), NS)
        if "pool" in parts:
            pool_mixer(NB, 4, poolc_s.all(), NS, first_chunk=False)
        if "nsa" in parts:
            nsa_sample(l, kvtok)
        for b in range(NB):
            for two in range(2):
                pr = slice(two * 64, (two + 1) * 64)
                P.dma(sre_s[l, b].rearrange("(m two) n -> two n m", two=2)[two], sst[pr, :, b, 0], is_output=True, allow_slow_non_contiguous=True)
                P.dma(sim_s[l, b].rearrange("(m two) n -> two n m", two=2)[two], sst[pr, :, b, 1], is_output=True, allow_slow_non_contiguous=True)
            for kcc in range(4):
                P.dma(conv_s[l, b][:, kcc * 128:(kcc + 1) * 128].rearrange("j p -> p j"), convc_s[:, kcc, b, :], is_output=True, allow_slow_non_contiguous=True)
                P.dma(pool_s[l, b][:, kcc * 128:(kcc + 1) * 128].rearrange("j p -> p j"), poolc_s[:, kcc, b, :], is_output=True, allow_slow_non_contiguous=True)
        if DEBUG:
            P.dma(d["dbgs"][:, 0:24], HT.all()[:, 0:24, 0:NS], eng="gpsimd", is_output=True)
        if "ffn" not in parts:
            return
        dense_tail(l, 1, NS, NS)
        dst = xs1d if l < n_layers - 1 else y_s
        P.dma(dst.all(), ACC.p(0)[0:NS, 0, :], is_output=(l == n_layers - 1))

    for l in range(n_layers):
        layer_setup(l)
        for c in range(n_chunks):
            prompt_chunk(l, c)
        if do_sample:
            sample_layer(l)
    P.emit()
    P.close()
    return nc, P

_CACHE = {}

def _consts():
    k = np.arange(128)
    caus = np.where(k[:, None] > k[None, :], NEG, 0.0).astype(np.float32)
    anti = np.where(k[:, None] <= k[None, :], NEG, 0.0).astype(np.float32)
    ebig = (np.arange(128)[:, None] == (np.arange(8192)[None, :] // 64)).astype(np.float32)
    j = np.arange(64)
    tmask = np.zeros((32, 128, 3, 64), np.float32)
    for T in range(32):
        qpos = T * 128 + np.arange(128)
        complete = ((j[None, :] + 1) * 64 - 1) <= qpos[:, None]
        cur = qpos // 64
        forced = (j[None, :] == 0) | (j[None, :] == cur[:, None]) | (j[None, :] == cur[:, None] - 1)
        started = (j[None, :] * 64) <= qpos[:, None]
        tmask[T, :, 0, :] = np.where(complete, 0.0, NEG)
        tmask[T, :, 1, :] = (started & ~forced).astype(np.float32)
        tmask[T, :, 2, :] = np.where(forced, 5.0, np.where(started, 0.0, -1.0))
    cmpbT = np.ascontiguousarray(tmask[:, :, 0, :].transpose(0, 2, 1))
    rc = np.zeros((128, 4, 16), np.float32)
    for kc, w in enumerate((2, 4, 8, 16)):
        rc[:, kc, :] = 1.0 / np.minimum(np.arange(16) + 1, w)
    NB = 32 // NCORES; NS = NB * 4
    ibig = (np.arange(254)[None, :] == (126 + np.arange(128)[:, None] // 64)).astype(np.float32)
    nb = np.full((NS, NB, 16), NEG, np.float32); selb = np.zeros((NS, NB, 16), np.float32)
    for b in range(NB):
        for r in range(4):
            for t in range(4):
                selb[4 * b + t, b, r * 4 + t] = 1.0
                for kt in range(t + 1):
                    nb[4 * b + kt, b, r * 4 + t] = 0.0
    wb0 = np.zeros((128, 16), np.float32)
    for r in range(4):
        for t in range(4):
            wb0[:t + 1, r * 4 + t] = NEG
    rmk = np.zeros((16, 6), np.float32)
    for r in range(4):
        for t in range(4):
            rmk[r * 4 + t, r % 2] = 1.0
            rmk[r * 4 + t, 2 + r] = 1.0
    extra = {"c_iota": np.arange(128, dtype=np.float32).reshape(128, 1), "c_ibig": ibig, "c_nb": nb, "c_wb0": wb0, "c_selb": selb, "c_rm": rmk}
    return {**extra, "c_ident": np.eye(128, dtype=np.float32), "c_caus": caus, "c_anti": anti, "c_ebig": ebig,
            "c_tmask": tmask, "c_cmpbT": cmpbT, "c_rcnt0": rc}


def _perm_w_in(w):
    w = np.array(w, dtype=np.float32, copy=True)
    q = w[:, :, 2560:3072].reshape(2, D, 8, 64)
    w[:, :, 2560:3072] = q[:, :, [0, 4, 1, 5, 2, 6, 3, 7]].reshape(2, D, 512)
    return w


def make_in_maps(inp):
    C = _consts()
    f = lambda a: np.ascontiguousarray(a)
    shared = {
        "cache": f(inp["cache_nsa_kv"]).reshape(2, 2560 * 128, 512)[:, 0:CACHE_ROWS],
        "w_in": _perm_w_in(inp["w_in"]), "w_out": f(inp["w_out"]), "w_up": f(inp["w_up"]), "w_down": f(inp["w_down"]),
        "a_re": f(inp["ssm_a_re"]), "a_im": f(inp["ssm_a_im"]), "ldt": f(inp["ssm_log_dt"]),
        "b_re": f(inp["ssm_b_re"]), "b_im": f(inp["ssm_b_im"]), "c_re": f(inp["ssm_c_re"]), "c_im": f(inp["ssm_c_im"]),
        "ssm_d": f(inp["ssm_d"]), "w_glu": f(inp["ssm_w_glu"]), "b_glu": f(inp["ssm_b_glu"]),
        "conv_w": f(inp["conv_w"]), "conv_b": f(inp["conv_b"]), "pool_w": f(inp["pool_w"]), "pool_sc": f(inp["pool_scale"]),
        "wck": f(inp["nsa_w_cmp_k"]), "wcv": f(inp["nsa_w_cmp_v"]),
        "ln1g": f(inp["ln1_g"]), "ln1b": f(inp["ln1_b"]), "ln2g": f(inp["ln2_g"]), "ln2b": f(inp["ln2_b"]),
    }
    shared.update(C)
    maps = []
    NB = 32 // NCORES
    for c in range(NCORES):
        b = c % 2
        sb = slice(NB * c, NB * c + NB)
        m = dict(shared)
        m["x_p"] = f(inp["x_prompt"][b])
        m["x_s"] = f(inp["x_sample"][sb]).reshape(NB * 4, D)
        m["cwin"] = f(inp["cache_win_kv"][:, sb]).reshape(2, NB, 512, 256)
        m["st_re"] = f(inp["state_ssm_re"][:, sb]); m["st_im"] = f(inp["state_ssm_im"][:, sb])
        m["st_conv"] = f(inp["state_conv"][:, sb]); m["st_pool"] = f(inp["state_pool"][:, sb])
        m["ptab"] = f(inp["page_table"][sb]).astype(np.int32)
        maps.append(m)
    return maps


def assemble(res):
    r = res
    y_p = np.stack([r[0]["y_p"], r[1]["y_p"]])
    cat = lambda key, ax: np.concatenate([r[c][key] for c in range(NCORES)], axis=ax)
    y_s = cat("y_s", 0).reshape(32, 4, D)
    kvp = np.stack([r[0]["kvp"], r[1]["kvp"]], axis=1).reshape(2, 2, SEQ, 4, 2, 64)
    kvs = cat("kvs", 1).reshape(2, 32, 4, 4, 2, 64)
    winp = np.stack([r[0]["winp"], r[1]["winp"]], axis=1).reshape(2, 2, 512, 2, 2, 64)
    wins = cat("wins", 1).reshape(2, 32, 512, 2, 2, 64)
    st2 = lambda key: np.stack([r[0][key], r[1][key]], axis=1)
    return (y_p, y_s, kvp, kvs, winp, wins, st2("sre_p"), st2("sim_p"), cat("sre_s", 1), cat("sim_s", 1),
            st2("conv_p"), cat("conv_s", 1), st2("pool_p"), cat("pool_s", 1))


def kernel(**inputs):
    if "nc" not in _CACHE:
        _CACHE["nc"] = build_program()[0]
    maps = make_in_maps(inputs)
    res = run_bass_kernel_spmd(_CACHE["nc"], maps, core_ids=list(range(NCORES)))
    return assemble(res.results)
```
